# Optimizing a Trainium2 kernel written in Bass

```python
import math
import jax, jax.numpy as jnp
from jax import lax
import numpy as np

D_MODEL = 4096
BATCH = 4
SEQ = 4096
DEPTH = 1

HEAD_DIM = 128
MOBA_HEADS = 16
FOX_HEADS = 16
MOBA_WIDTH = MOBA_HEADS * HEAD_DIM
FOX_WIDTH = FOX_HEADS * HEAD_DIM
MOBA_BLOCK = 256
MOBA_TOPK = 3
MOBA_Q_CHUNK = 16
FOX_Q_BLOCK = 128
T5_NUM_BUCKETS = 32
T5_MAX_DISTANCE = 128
LN_EPS = 1e-5
FORGET_BIAS_INIT = 3.0
DEEPNORM_ALPHA = (2.0 * DEPTH) ** 0.25
DEEPNORM_BETA = (8.0 * DEPTH) ** -0.25
IN_SIZES = [MOBA_WIDTH] * 4 + [FOX_WIDTH] * 4 + [FOX_HEADS, 2 * D_MODEL]
IN_COLS = int(sum(IN_SIZES))
IN_SPLITS = [int(v) for v in np.cumsum(IN_SIZES)[:-1]]

kernel_name = "moba_fox_gated_hybrid_deepnorm"


def t5_bucket(dist):
    max_exact = T5_NUM_BUCKETS // 2
    d = jnp.maximum(dist, 1).astype(jnp.float32)
    large = max_exact + (jnp.log(d / max_exact) / math.log(T5_MAX_DISTANCE / max_exact)
                         * (T5_NUM_BUCKETS - max_exact)).astype(jnp.int32)
    large = jnp.minimum(large, T5_NUM_BUCKETS - 1)
    return jnp.where(dist < max_exact, dist, large)


def moba_attention(q, k, v, rel_bias_table):
    B, H, S, Dh = q.shape
    nb = -(-S // MOBA_BLOCK)
    pad = nb * MOBA_BLOCK - S
    kp = jnp.pad(k, ((0, 0), (0, 0), (0, pad), (0, 0)))
    vp = jnp.pad(v, ((0, 0), (0, 0), (0, pad), (0, 0)))
    kb = kp.reshape(B, H, nb, MOBA_BLOCK, Dh)
    vb = vp.reshape(B, H, nb, MOBA_BLOCK, Dh)
    k_mean = jnp.mean(kb.astype(jnp.float32), axis=3)
    bias_hb = rel_bias_table.T.astype(jnp.float32)
    scale = HEAD_DIM ** -0.5
    topk = min(MOBA_TOPK, nb)
    C = MOBA_Q_CHUNK
    n_chunks = S // C
    qc = q.reshape(B, H, n_chunks, C, Dh).transpose(2, 0, 1, 3, 4)
    b_idx = jnp.arange(B)[:, None, None, None]
    h_idx = jnp.arange(H)[None, :, None, None]
    blk_ar = jnp.arange(MOBA_BLOCK, dtype=jnp.int32)

    def chunk(args):
        ci, qi = args
        q_pos = ci * C + jnp.arange(C, dtype=jnp.int32)
        own = (ci * C) // MOBA_BLOCK
        gate = jnp.einsum("bhcd,bhnd->bhcn", qi.astype(jnp.float32), k_mean)
        past = jnp.arange(nb)[None, :] < own
        gate = jnp.where(past[None, None], gate, -1e30)
        _, sel = lax.top_k(gate, topk)
        slot_valid = jnp.arange(topk) < own
        sel = jnp.where(slot_valid, sel, 0)
        k_sel = kb[b_idx, h_idx, sel].reshape(B, H, C, topk * MOBA_BLOCK, Dh)
        v_sel = vb[b_idx, h_idx, sel].reshape(B, H, C, topk * MOBA_BLOCK, Dh)
        s_sel = jnp.einsum("bhcd,bhckd->bhck", qi, k_sel).astype(jnp.float32) * scale
        key_pos_sel = (sel[..., None] * MOBA_BLOCK + blk_ar).reshape(B, H, C, topk * MOBA_BLOCK)
        dist_sel = jnp.maximum(q_pos[:, None] - key_pos_sel, 0)
        s_sel = s_sel + bias_hb[h_idx, t5_bucket(dist_sel)]
        valid_sel = jnp.repeat(slot_valid, MOBA_BLOCK)
        s_sel = jnp.where(valid_sel, s_sel, -jnp.inf)
        k_own = lax.dynamic_index_in_dim(kb, own, axis=2, keepdims=False)
        v_own = lax.dynamic_index_in_dim(vb, own, axis=2, keepdims=False)
        s_own = jnp.einsum("bhcd,bhkd->bhck", qi, k_own).astype(jnp.float32) * scale
        dist_own = q_pos[:, None] - (own * MOBA_BLOCK + blk_ar)[None, :]
        s_own = s_own + bias_hb[:, t5_bucket(jnp.maximum(dist_own, 0))]
        s_own = jnp.where(dist_own >= 0, s_own, -jnp.inf)
        p = jax.nn.softmax(jnp.concatenate([s_sel, s_own], axis=-1), axis=-1).astype(v.dtype)
        p_sel, p_own = p[..., :topk * MOBA_BLOCK], p[..., topk * MOBA_BLOCK:]
        return (jnp.einsum("bhck,bhckd->bhcd", p_sel, v_sel)
                + jnp.einsum("bhck,bhkd->bhcd", p_own, v_own))

    out = lax.map(chunk, (jnp.arange(n_chunks, dtype=jnp.int32), qc))
    return out.transpose(1, 2, 0, 3, 4).reshape(B, H, S, Dh)


def forgetting_attention(q, k, v, log_f):
    B, H, S, Dh = q.shape
    c = jnp.cumsum(log_f, axis=-1)
    nqb = S // FOX_Q_BLOCK
    qb = q.reshape(B, H, nqb, FOX_Q_BLOCK, Dh).transpose(2, 0, 1, 3, 4)
    cb = c.reshape(B, H, nqb, FOX_Q_BLOCK).transpose(2, 0, 1, 3)
    key_pos = jnp.arange(S, dtype=jnp.int32)
    scale = HEAD_DIM ** -0.5

    def block(args):
        bi, qi, ci = args
        q_pos = bi * FOX_Q_BLOCK + jnp.arange(FOX_Q_BLOCK, dtype=jnp.int32)
        s = jnp.einsum("bhqd,bhkd->bhqk", qi, k).astype(jnp.float32) * scale
        s = s + ci[..., None] - c[:, :, None, :]
        s = jnp.where(key_pos[None, :] <= q_pos[:, None], s, -jnp.inf)
        p = jax.nn.softmax(s, axis=-1).astype(v.dtype)
        return jnp.einsum("bhqk,bhkd->bhqd", p, v)

    out = lax.map(block, (jnp.arange(nqb, dtype=jnp.int32), qb, cb))
    return out.transpose(1, 2, 0, 3, 4).reshape(B, H, S, Dh)


def layer_norm(h, gain, bias):
    hf = h.astype(jnp.float32)
    mu = jnp.mean(hf, axis=-1, keepdims=True)
    var = jnp.mean(jnp.square(hf - mu), axis=-1, keepdims=True)
    return ((hf - mu) * lax.rsqrt(var + LN_EPS) * gain + bias).astype(h.dtype)


def setup_inputs(seed: int = 0) -> dict:
    key = jax.random.key(seed)
    ks = jax.random.split(key, 10)
    x = jax.random.normal(ks[0], (BATCH, SEQ, D_MODEL), jnp.float32)
    col_scale = jnp.concatenate([
        jnp.ones((2 * MOBA_WIDTH,)), jnp.full((MOBA_WIDTH,), DEEPNORM_BETA), jnp.ones((MOBA_WIDTH,)),
        jnp.ones((2 * FOX_WIDTH,)), jnp.full((FOX_WIDTH,), DEEPNORM_BETA), jnp.ones((FOX_WIDTH,)),
        jnp.ones((FOX_HEADS + 2 * D_MODEL,))]).astype(jnp.float32)
    w_in = jax.random.normal(ks[1], (DEPTH, D_MODEL, IN_COLS), jnp.float32) * D_MODEL ** -0.5 * col_scale
    b_forget = FORGET_BIAS_INIT + 0.1 * jax.random.normal(ks[2], (DEPTH, FOX_HEADS), jnp.float32)
    b_gate = 0.1 * jax.random.normal(ks[3], (DEPTH, 2, D_MODEL), jnp.float32)
    rel_bias_table = 0.5 * jax.random.normal(ks[4], (T5_NUM_BUCKETS, MOBA_HEADS), jnp.float32)
    w_branch = (jax.random.normal(ks[5], (DEPTH, 2, MOBA_WIDTH, D_MODEL), jnp.float32)
                * MOBA_WIDTH ** -0.5 * DEEPNORM_BETA)
    w_out = jax.random.normal(ks[6], (DEPTH, D_MODEL, D_MODEL), jnp.float32) * D_MODEL ** -0.5 * DEEPNORM_BETA
    ln_gain = 1.0 + 0.1 * jax.random.normal(ks[7], (DEPTH, D_MODEL), jnp.float32)
    ln_bias = 0.1 * jax.random.normal(ks[8], (DEPTH, D_MODEL), jnp.float32)
    return {"x": x, "w_in": w_in, "b_forget": b_forget, "b_gate": b_gate,
            "rel_bias_table": rel_bias_table, "w_branch": w_branch, "w_out": w_out,
            "ln_gain": ln_gain, "ln_bias": ln_bias}


def reference(x, w_in, b_forget, b_gate, rel_bias_table, w_branch, w_out, ln_gain, ln_bias):
    B, S, D = x.shape

    def heads(t, H):
        return t.reshape(B, S, H, HEAD_DIM).transpose(0, 2, 1, 3)

    def merge_heads(t):
        return t.transpose(0, 2, 1, 3).reshape(B, S, -1)

    for layer in range(DEPTH):
        proj = jnp.einsum("bsd,de->bse", x, w_in[layer])
        qa, ka, va, za, qf, kf, vf, zf, f_logit, g_logit = jnp.split(proj, IN_SPLITS, axis=-1)
        ya = merge_heads(moba_attention(heads(qa, MOBA_HEADS), heads(ka, MOBA_HEADS),
                                        heads(va, MOBA_HEADS), rel_bias_table))
        ya = ya * jax.nn.silu(za)
        log_f = jax.nn.log_sigmoid(f_logit.astype(jnp.float32) + b_forget[layer]).transpose(0, 2, 1)
        yf = merge_heads(forgetting_attention(heads(qf, FOX_HEADS), heads(kf, FOX_HEADS),
                                              heads(vf, FOX_HEADS), log_f))
        yf = yf * jax.nn.silu(zf)
        ua = jnp.einsum("bsw,wd->bsd", ya, w_branch[layer, 0])
        uf = jnp.einsum("bsw,wd->bsd", yf, w_branch[layer, 1])
        gates = jax.nn.sigmoid(g_logit.reshape(B, S, 2, D) + b_gate[layer])
        merged = gates[:, :, 0] * ua + gates[:, :, 1] * uf
        out = jnp.einsum("bsd,de->bse", merged, w_out[layer])
        x = layer_norm(DEEPNORM_ALPHA * x + out, ln_gain[layer], ln_bias[layer])
    return x
```

```python
from contextlib import ExitStack
from concourse.bass_utils import run_bass_kernel_spmd
import numpy as np
import concourse.bass as bass
import concourse.mybir as mybir

F32 = mybir.dt.float32
BF16 = mybir.dt.bfloat16
AF = mybir.ActivationFunctionType
ALU = mybir.AluOpType
AX = mybir.AxisListType


class Sem:
    _n = 0

    def __init__(self, h):
        self.h = h
        Sem._n += 1
        self.id = Sem._n


class Buf:
    def __init__(self, name, dsem=None):
        self.name = name
        self.ew = None
        self.er = {}
        self.dsem = dsem
        self.dcnt = 0


class Eng:
    def __init__(self, fw, name, eng, sem, same_sync):
        self.fw = fw
        self.name = name
        self.e = eng
        self.sem = sem
        self.n = 0
        self.waited = {}
        self.same_sync = same_sync
        self.thunks = []

    def _wait(self, sem, val):
        if val <= 0:
            return
        if self.waited.get(sem.id, 0) >= val:
            return
        self.waited[sem.id] = val
        e = self.e
        h = sem.h
        self.thunks.append(lambda: e.wait_ge(h, val))

    def _dep_eng(self, w):
        if w is None:
            return
        eng, n = w
        if eng is self and not self.same_sync:
            return
        self._wait(eng.sem, n)

    def deps(self, reads, writes):
        for b in reads:
            self._dep_eng(b.ew)
            if b.dsem is not None and b.dcnt:
                self._wait(b.dsem, 16 * b.dcnt)
        for b in writes:
            self._dep_eng(b.ew)
            for eng, n in b.er.items():
                self._dep_eng((eng, n))
            if b.dsem is not None and b.dcnt:
                self._wait(b.dsem, 16 * b.dcnt)

    def op(self, method, kw, reads=(), writes=(), track=True):
        self.deps(reads, writes)
        if track:
            self.n += 1
            n = self.n
            h = self.sem.h
            self.thunks.append(lambda: method(**kw).then_inc(h, 1))
        else:
            n = self.n + 1
            self.thunks.append(lambda: method(**kw))
        for b in reads:
            if b.er.get(self, 0) < n:
                b.er[self] = n
        for b in writes:
            b.ew = (self, n)
            b.er = {}

    def dma(self, out, in_, reads=(), writes=(), free_owner=None, **kw):
        self.deps(reads, writes)
        owner = free_owner
        for b in list(writes) + list(reads):
            if owner is None and b.dsem is not None:
                owner = b
                break
        assert owner is not None
        owner.dcnt += 1
        h = owner.dsem.h
        e = self.e
        self.thunks.append(lambda: e.dma_start(out=out, in_=in_, **kw).then_inc(h, 16))
        for b in writes:
            if b is not owner:
                raise AssertionError("multi-buffer dma write")
            b.ew = None
            b.er = {}

    def wait_dma(self, b):
        if b.dsem is not None and b.dcnt:
            self._wait(b.dsem, 16 * b.dcnt)


class FW:
    def __init__(self, nc, stack, same_sync=True):
        self.nc = nc
        self.stack = stack
        self.semstack = stack
        self.sems = []
        self.dbufs = []

        def mk(name, eng, ss):
            return Eng(self, name, eng, self.sem(name), ss)
        self.pe = mk("pe", nc.tensor, False)
        self.act = mk("act", nc.scalar, same_sync)
        self.dve = mk("dve", nc.vector, same_sync)
        self.pool = mk("pool", nc.gpsimd, same_sync)
        self.sp = mk("sp", nc.sync, False)
        self.engs = [self.pe, self.act, self.dve, self.pool, self.sp]

    def sem(self, name):
        h = self.semstack.enter_context(self.nc.semaphore(name))
        s = Sem(h)
        self.sems.append(s)
        return s

    def sb(self, name, shape, dt):
        return self.stack.enter_context(self.nc.sbuf_tensor("s_" + name, shape, dt))

    def ps(self, name, shape, dt):
        return self.stack.enter_context(self.nc.psum_tensor(name, shape, dt))

    def buf(self, name, dma=False):
        b = Buf(name, self.sem("d_" + name) if dma else None)
        if dma:
            self.dbufs.append(b)
        return b

    def barrier(self):
        for e in self.engs:
            for o in self.engs:
                if o is not e and o.n:
                    e._wait(o.sem, o.n)
            for b in self.dbufs:
                e.wait_dma(b)

    def replay(self):
        nc = self.nc
        with nc.Block() as block:
            @block.tensor
            def _(e):
                for t in self.pe.thunks:
                    t()

            @block.scalar
            def _(e):
                for t in self.act.thunks:
                    t()

            @block.vector
            def _(e):
                for t in self.dve.thunks:
                    t()

            @block.gpsimd
            def _(e):
                for t in self.pool.thunks:
                    t()

            @block.sync
            def _(e):
                for t in self.sp.thunks:
                    t()


D = 4096
NOWN = 2048
NCTX = 2048
NTOK = NOWN + NCTX
NCOL = 24592
GCOL0 = 16400
SCALE = 128 ** -0.5
ALPHA = 2.0 ** 0.25
EPS = 1e-5
NEG = -30000.0


def build_program(debug=False, phases=(1, 2, 3), debug_heads=(0, 16), p2stop=9):
    nc = bass.Bass("TRN2", target_bir_lowering=False)

    def din(name, shape, dt=F32):
        return nc.dram_tensor(name, shape, dt, kind="ExternalInput").ap()

    def dscr(name, shape, dt):
        kind = "ExternalOutput" if debug else "Internal"
        return nc.dram_tensor(name, shape, dt, kind=kind).ap()

    xo = din("xo", [NOWN, D])
    xc = din("xc", [NCTX, D])
    w_in = din("w_in", [D if 1 in phases else 1, NCOL])
    nbfT = din("bfT", [16, 1])
    bgT = din("bgT", [128, 64])
    tb = din("tb", [16, 2, 128, 128])
    b31 = din("b31", [128, 16])
    w_br = din("w_br", [2, 2048 if 3 in phases else 1, D])
    w_out = din("w_out", [D if 3 in phases else 1, D])
    lng = din("lng", [128, D])
    lnb = din("lnb", [128, D])
    ctxm_d = din("ctxm", [128, 1])
    out = nc.dram_tensor("out", [NOWN, D], F32, kind="ExternalOutput").ap()

    S1 = dscr("S1", [16384, NOWN], BF16)
    SC = dscr("SC", [8192, NCTX], BF16)
    SG = dscr("SG", [8192, NOWN], F32)
    YG = dscr("YG", [4096, NOWN], BF16)
    WBR16 = nc.dram_tensor("WBR16", [4096, D], BF16, kind="Internal").ap()
    WO16 = nc.dram_tensor("WO16", [D, D], BF16, kind="Internal").ap()
    CTd = dscr("CTd", [16, NTOK], F32) if debug else None

    with ExitStack() as st:
        fw = FW(nc, st)
        pe, act, dve, pool, sp = fw.pe, fw.act, fw.dve, fw.pool, fw.sp
        T, V, A, G = nc.tensor, nc.vector, nc.scalar, nc.gpsimd

        banks = [fw.ps(f"bank{i}", [128, 512], F32) for i in range(8)]
        bbanks = [fw.buf(f"bank{i}") for i in range(8)]
        bankctr = [0]

        def nextbank():
            i = bankctr[0] % 8
            bankctr[0] += 1
            return banks[i], bbanks[i]

        identf = fw.sb("identf", [128, 128], F32)
        identb = fw.sb("identb", [128, 128], BF16)
        bconst = fw.buf("const")
        cT = fw.sb("cT", [16, NTOK], F32)
        bcT = fw.buf("cT", dma=True)
        nbf = fw.sb("nbf", [16, 1], F32)
        bsmall = fw.buf("small", dma=True)
        bgs = fw.sb("bgs", [128, 64], F32)
        b31b = fw.sb("b31b", [128, 16], F32)
        ctxm = fw.sb("ctxm", [128, 1], F32)

        bwconv = fw.buf("wconv", dma=True)
        sp.dma(nbf[:], nbfT, writes=[bsmall])
        sp.dma(bgs[:], bgT, writes=[bsmall])
        sp.dma(b31b[:], b31, writes=[bsmall])
        sp.dma(ctxm[:], ctxm_d, writes=[bsmall])
        dve.op(V.tensor_scalar, dict(out=nbf[:], in0=nbf[:], scalar1=-1.0, scalar2=None, op0=ALU.mult),
               reads=[bsmall], writes=[bsmall])
        pool.op(G.memset, dict(ap=identf[:], constant=1.0), writes=[bconst])
        pool.op(G.affine_select, dict(out=identf[:], in_=identf[:], pattern=[[-1, 128]],
                                      compare_op=ALU.is_equal, fill=0.0, base=0, channel_multiplier=1),
                reads=[bconst], writes=[bconst])
        dve.op(V.tensor_copy, dict(out=identb[:], in_=identf[:]), reads=[bconst], writes=[bconst])

        if 1 in phases:
            with ExitStack() as p1:
                fw.stack = p1
                TT = 1024
                xT = fw.sb("xT", [128, 32, TT], BF16)
                bxT = [[fw.buf(f"xT{ts}_{kb}") for kb in range(4)] for ts in range(8)]
                wring = [fw.sb(f"wr{i}", [128, 32, 512], BF16) for i in range(2)]
                bw = [fw.buf(f"wr{i}", dma=True) for i in range(2)]
                xld = [fw.sb(f"xld{i}", [128, D], BF16) for i in range(3)]
                bxld = [fw.buf(f"xld{i}", dma=True) for i in range(3)]
                stgb = [fw.sb(f"stgb{i}", [128, 4, 512], BF16) for i in range(2)]
                bstgb = [fw.buf(f"stgb{i}", dma=True) for i in range(2)]
                stgf = [fw.sb(f"stgf{i}", [128, 4, 512], F32) for i in range(2)]
                bstgf = [fw.buf(f"stgf{i}", dma=True) for i in range(2)]
                wf = fw.sb("wf", [128, 32, 16], BF16)
                bwf = fw.buf("wf", dma=True)
                lft = fw.sb("lft", [16, TT], F32)
                blft = fw.buf("lft")
                ones16 = fw.sb("ones16", [16, TT], F32)
                bones16 = fw.buf("ones16")
                dve.op(V.memset, dict(ap=ones16[:], constant=1.0), writes=[bones16])
                pool.dma(wf[:], w_in[:, 16384:16400].rearrange("(k p) c -> p k c", p=128), writes=[bwf])

                own_tiles = []
                kinds = ["q", "k", "v", "z", "q", "k", "v", "z"]
                for sec in range(8):
                    for i in range(4):
                        c0 = sec * 2048 + i * 512
                        own_tiles.append((c0, kinds[sec], S1, c0))
                for i in range(16):
                    own_tiles.append((GCOL0 + i * 512, "g", SG, i * 512))
                ctx_tiles = []
                for wi, sec in enumerate([1, 2, 5, 6]):
                    for i in range(4):
                        c0 = sec * 2048 + i * 512
                        ctx_tiles.append((c0, "k", SC, wi * 2048 + i * 512))

                wcount = 0
                scount = [0, 0]
                evtog = 0
                for tt in range(4):
                    is_ctx = tt < 2
                    src = xc if is_ctx else xo
                    row0 = (tt % 2) * TT
                    gtok0 = tt * TT
                    def build_sub(ts):
                        nonlocal evtog
                        b = ts % 3
                        pool.dma(xld[b][:], src[row0 + ts * 128: row0 + (ts + 1) * 128, :], writes=[bxld[b]])
                        for kb in range(4):
                            bank, bb = nextbank()
                            bankb = bank[:].bitcast(BF16)
                            for kk in range(8):
                                k = kb * 8 + kk
                                pe.op(T.transpose, dict(out=bankb[:, kk * 128:(kk + 1) * 128],
                                                        in_=xld[b][:, k * 128:(k + 1) * 128], identity=identb[:]),
                                      reads=[bxld[b], bconst], writes=[bb], track=(kk == 7))
                            eng, meth = (act, A.copy) if (evtog % 2 == 0) else (dve, V.tensor_copy)
                            evtog += 1
                            eng.op(meth, dict(out=xT[:, kb * 8:(kb + 1) * 8, ts * 128:(ts + 1) * 128],
                                              in_=bankb.rearrange("p (k t) -> p k t", k=8)),
                                   reads=[bb], writes=[bxT[ts][kb]])

                    def f_chunk(c):
                        bank, bb = nextbank()
                        for k in range(32):
                            pe.op(T.matmul, dict(out=bank[0:16, :], lhsT=wf[:, k, :], rhs=xT[:, k, c * 512:(c + 1) * 512],
                                                 start=(k == 0), stop=(k == 31)),
                                  reads=[bwf] + [bxT[c * 4 + t4][k // 8] for t4 in range(4)], writes=[bb], track=(k == 31))
                        act.op(A.activation, dict(out=lft[:, c * 512:(c + 1) * 512], in_=bank[0:16, :], func=AF.Exp,
                                                  bias=nbf[:, 0:1], scale=-1.0),
                               reads=[bb, bsmall], writes=[blft])

                    def f_finish():
                        act.op(A.activation, dict(out=lft[:], in_=lft[:], func=AF.Ln, bias=1.0, scale=1.0),
                               reads=[blft], writes=[blft])
                        init = 0.0 if tt == 0 else cT[:, gtok0 - 1:gtok0]
                        dve.op(V.tensor_tensor_scan, dict(out=cT[:, gtok0:gtok0 + TT], data0=ones16[:], data1=lft[:],
                                                          initial=init, op0=ALU.mult, op1=ALU.subtract),
                               reads=[blft, bones16, bcT], writes=[bcT])

                    def gemm_group(slot, kind, c0, c, g, stg, bstg):
                        bank, bb = nextbank()
                        for k in range(32):
                            pe.op(T.matmul, dict(out=bank[:], lhsT=wring[slot][:, k, g * 128:(g + 1) * 128],
                                                 rhs=xT[:, k, c * 512:(c + 1) * 512],
                                                 start=(k == 0), stop=(k == 31)),
                                  reads=[bw[slot]] + [bxT[c * 4 + t4][k // 8] for t4 in range(4)],
                                  writes=[bb], track=(k == 31))
                        if kind == "q":
                            act.op(A.activation, dict(out=stg[:, g, :], in_=bank[:], func=AF.Copy, scale=SCALE),
                                   reads=[bb], writes=[bstg])
                        elif kind == "z":
                            act.op(A.activation, dict(out=stg[:, g, :], in_=bank[:], func=AF.Silu),
                                   reads=[bb], writes=[bstg])
                        elif kind == "g":
                            gi = (c0 - GCOL0) // 128 + g
                            act.op(A.activation, dict(out=stg[:, g, :], in_=bank[:], func=AF.Sigmoid,
                                                      bias=bgs[:, gi:gi + 1], scale=1.0),
                                   reads=[bb, bsmall], writes=[bstg])
                        else:
                            dve.op(V.tensor_copy, dict(out=stg[:, g, :], in_=bank[:]), reads=[bb], writes=[bstg])

                    for ts in range(4):
                        build_sub(ts)
                    f_chunk(0)
                    tiles = ctx_tiles if is_ctx else own_tiles
                    for ti, (c0, kind, dst, drow0) in enumerate(tiles):
                        slot = wcount % 2
                        wcount += 1
                        pool.dma(wring[slot][:], w_in[:, c0:c0 + 512].rearrange("(k p) c -> p k c", p=128),
                                 writes=[bw[slot]])
                        for c in range(2):
                            isg = kind == "g"
                            si = 1 if isg else 0
                            ss = scount[si] % 2
                            scount[si] += 1
                            stg, bstg = (stgf[ss], bstgf[ss]) if isg else (stgb[ss], bstgb[ss])
                            if ti == 0 and c == 1:
                                f_chunk(1)
                                f_finish()
                            for g in range(4):
                                if ti == 0 and c == 0:
                                    build_sub(4 + g)
                                gemm_group(slot, kind, c0, c, g, stg, bstg)
                            t0 = row0 + c * 512
                            sp.dma(dst[drow0:drow0 + 512, t0:t0 + 512].rearrange("(g p) t -> p g t", p=128), stg[:],
                                   reads=[bstg])
                if debug:
                    sp.dma(CTd, cT[:], reads=[bcT])
                fw.barrier()
                fw.stack = st

        if 2 in phases:
            fw.barrier()
            with ExitStack() as p2:
                fw.stack = p2
                SB = [banks[0], banks[1], banks[2], banks[3]]
                BSB = [bbanks[0], bbanks[1], bbanks[2], bbanks[3]]
                PY = [(banks[4], bbanks[4]), (banks[5], bbanks[5])]
                PSM = (banks[6], bbanks[6])
                MBs = [(banks[7], bbanks[7]), (banks[7], bbanks[7])]
                MB, BMB = banks[7], bbanks[7]

                KT = [fw.sb(f"KT{i}", [128, NTOK], BF16) for i in range(3)]
                VT = [fw.sb(f"VT{i}", [128, NTOK], BF16) for i in range(3)]
                QT = [fw.sb(f"QT{i}", [128, NOWN], BF16) for i in range(3)]
                ZT = [fw.sb(f"ZT{i}", [128, NOWN], BF16) for i in range(3)]
                bKT = [fw.buf(f"KT{i}", dma=True) for i in range(3)]
                bVT = [fw.buf(f"VT{i}", dma=True) for i in range(3)]
                bQT = [fw.buf(f"QT{i}", dma=True) for i in range(3)]
                bZT = [fw.buf(f"ZT{i}", dma=True) for i in range(3)]
                Vtok = [fw.sb(f"Vtok{q}", [128, 32, 128], BF16) for q in range(2)]
                bVtok = [[fw.buf(f"Vtok{q}_{i}") for i in range(4)] for q in range(2)]
                NPT = 8
                PT = [fw.sb(f"PT{i}", [128, 512], BF16) for i in range(NPT)]
                bPT = [fw.buf(f"PT{i}") for i in range(NPT)]
                ygst = [fw.sb(f"ygst{i}", [128, NOWN], BF16) for i in range(2)]
                bygst = [fw.buf(f"ygst{i}", dma=True) for i in range(2)]
                rc = fw.sb("rc", [128, 512], F32)
                yt = fw.sb("yt", [128, 512], F32)
                brc = fw.buf("rc")
                byt = fw.buf("yt")
                onesb = fw.sb("onesb", [128, 128], BF16)
                onesf = fw.sb("onesf", [128, 128], F32)
                cm = fw.sb("cm", [128, 128], F32)
                tbx = fw.sb("tbx", [128, 32, 128], F32)
                btbx = fw.buf("tbx", dma=True)
                self_f = fw.sb("self", [16, 16, 128], F32)
                selb = fw.sb("selb", [128, 16, 128], BF16)
                gm = fw.sb("gm", [128, 16, 16], F32)
                ccolT = fw.sb("ccolT", [128, 32, 16], F32)
                crefb = fw.sb("crefb", [128, 16, 4], F32)
                fb = [fw.sb(f"fb{q}", [128, 32, 4], F32) for q in range(2)]
                bfb = [fw.buf(f"fb{q}") for q in range(2)]
                bsetup = fw.buf("setup")
                ks = fw.sb("ks", [128, 16], F32)
                kshi = fw.sb("kshi", [128, 16], BF16)
                kslo = fw.sb("kslo", [128, 16], BF16)
                bks = fw.buf("ks")
                gsb = fw.sb("gsb", [128, 256], F32)
                top8 = fw.sb("top8", [128, 128], F32)
                mb = fw.sb("mb", [128, 256], F32)
                mb2 = fw.sb("mb2", [128, 256], F32)
                bg = fw.buf("gating")
                MT = [fw.sb(f"MT{q}", [128, NOWN], BF16) for q in range(2)]
                bMT = [fw.buf(f"MT{q}") for q in range(2)]

                dve.op(V.memset, dict(ap=onesb[:], constant=1.0), writes=[bsetup])
                dve.op(V.memset, dict(ap=onesf[:], constant=1.0), reads=[bsetup], writes=[bsetup])
                pool.op(G.memset, dict(ap=cm[:], constant=0.0), writes=[bsetup])
                pool.op(G.affine_select, dict(out=cm[:], in_=cm[:], pattern=[[1, 128]], compare_op=ALU.is_ge,
                                              fill=NEG, base=0, channel_multiplier=-1),
                        reads=[bsetup], writes=[bsetup])
                pool.op(G.memset, dict(ap=self_f[:], constant=1.0), writes=[bsetup])
                for n in range(16):
                    pool.op(G.affine_select, dict(out=self_f[:, n, :], in_=self_f[:, n, :], pattern=[[0, 128]],
                                                  compare_op=ALU.is_equal, fill=0.0, base=-n, channel_multiplier=1),
                            reads=[bsetup], writes=[bsetup])
                w_br_flat = w_br.rearrange("b w d -> (b w) d")

                def convert_weights(i):
                    if 3 not in phases or i >= 64:
                        return
                    if i < 32:
                        pool.dma(WBR16[i * 128:(i + 1) * 128, :], w_br_flat[i * 128:(i + 1) * 128, :], free_owner=bwconv)
                    else:
                        j = i - 32
                        pool.dma(WO16[j * 128:(j + 1) * 128, :], w_out[j * 128:(j + 1) * 128, :], free_owner=bwconv)
                dve.op(V.memset, dict(ap=selb[:], constant=0.0), reads=[bsetup], writes=[bsetup])
                dve.op(V.tensor_copy, dict(out=selb[0:16], in_=self_f[:]), reads=[bsetup], writes=[bsetup])
                for q in range(2):
                    dve.op(V.memset, dict(ap=MT[q][:], constant=0.0), writes=[bMT[q]])
                dve.op(V.memset, dict(ap=gm[:], constant=0.0), reads=[bsetup], writes=[bsetup])
                for i in range(16):
                    n0 = 8 + i // 2
                    dve.op(V.memset, dict(ap=gm[:, i, n0:16], constant=-1e30), reads=[bsetup], writes=[bsetup])
                for i in range(16):
                    dve.op(V.tensor_scalar, dict(out=gm[:, i, 0:8], in0=gm[:, i, 0:8], scalar1=ctxm[:, 0:1], scalar2=None,
                                                 op0=ALU.add), reads=[bsetup, bsmall], writes=[bsetup])
                def setup_tbx():
                    for h in range(16):
                        dve.op(V.tensor_scalar, dict(out=tbx[:, 2 * h:2 * h + 2, :], in0=tbx[:, 2 * h:2 * h + 2, :],
                                                     scalar1=b31b[:, h:h + 1], scalar2=None, op0=ALU.subtract),
                               reads=[btbx, bsmall], writes=[btbx])
                        dve.op(V.tensor_tensor, dict(out=tbx[:, 2 * h, :], in0=tbx[:, 2 * h, :], in1=cm[:], op=ALU.add),
                               reads=[btbx, bsetup], writes=[btbx])
                for j in range(32):
                    pe.op(T.matmul, dict(out=MB[:, j * 16:(j + 1) * 16], lhsT=cT[:, j * 128:(j + 1) * 128],
                                         rhs=identf[0:16, 0:16], start=True, stop=True),
                          reads=[bcT, bconst], writes=[BMB], track=(j == 31))
                dve.op(V.tensor_copy, dict(out=ccolT[:].rearrange("p j h -> p (j h)"), in_=MB[:, 0:512]),
                       reads=[BMB], writes=[bsetup])
                for h in range(16):
                    pe.op(T.matmul, dict(out=MB[:, h * 4:(h + 1) * 4], lhsT=self_f[:, h, :],
                                         rhs=cT[:, NCTX:NTOK].rearrange("p (c s) -> p c s", s=512)[:, :, 0], start=True, stop=True),
                          reads=[bcT, bsetup], writes=[BMB], track=(h == 15))
                dve.op(V.tensor_copy, dict(out=crefb[:].rearrange("p h c -> p (h c)"), in_=MB[:, 0:64]),
                       reads=[BMB], writes=[bsetup])

                def load_head(hd, slot):
                    br, hh = hd // 16, hd % 16
                    base = br * 8192
                    qrow = base + hh * 128
                    krow = base + 2048 + hh * 128
                    vrow = base + 4096 + hh * 128
                    zrow = base + 6144 + hh * 128
                    ckrow = (2 * br) * 2048 + hh * 128
                    cvrow = (2 * br + 1) * 2048 + hh * 128
                    sp.dma(VT[slot][:, 0:NCTX], SC[cvrow:cvrow + 128, :], writes=[bVT[slot]])
                    sp.dma(VT[slot][:, NCTX:NTOK], S1[vrow:vrow + 128, :], free_owner=bVT[slot])
                    sp.dma(KT[slot][:, 0:NCTX], SC[ckrow:ckrow + 128, :], writes=[bKT[slot]])
                    sp.dma(KT[slot][:, NCTX:NTOK], S1[krow:krow + 128, :], free_owner=bKT[slot])
                    sp.dma(QT[slot][:], S1[qrow:qrow + 128, :], writes=[bQT[slot]])
                    sp.dma(ZT[slot][:], S1[zrow:zrow + 128, :], writes=[bZT[slot]])

                heads = list(range(32)) if not debug else list(debug_heads)
                if p2stop == 0:
                    heads = []
                NH = len(heads)
                if NH:
                    load_head(heads[0], 0)
                    if NH > 1:
                        load_head(heads[1], 1)
                sp.dma(tbx[:], tb.rearrange("h r s t -> s (h r) t"), writes=[btbx])

                def slot_of(hi):
                    s3 = hi % 3
                    return (KT[s3], VT[s3], QT[s3], ZT[s3], bKT[s3], bVT[s3], bQT[s3], bZT[s3])

                def prepA(hi):
                    hd = heads[hi]
                    moba = hd < 16
                    hh = hd % 16
                    kt, vt, qt, zt, bkt, bvt, bqt, bzt = slot_of(hi)
                    p = hi % 2
                    MBk, BMBk = MBs[p]
                    for jb in range(4):
                        mbb = MBk[:].bitcast(BF16)
                        for jj in range(8):
                            j = jb * 8 + jj
                            pe.op(T.transpose, dict(out=mbb[:, jj * 128:(jj + 1) * 128], in_=vt[:, j * 128:(j + 1) * 128],
                                                    identity=identb[:]),
                                  reads=[bvt, bconst], writes=[BMBk], track=(jj == 7))
                        dve.op(V.tensor_copy, dict(out=Vtok[p][:, jb * 8:(jb + 1) * 8, :],
                                                   in_=mbb.rearrange("p (j d) -> p j d", j=8)),
                               reads=[BMBk], writes=[bVtok[p][jb]])
                    if moba:
                        dve.op(V.tensor_reduce, dict(out=ks[:], in_=kt[:].rearrange("p (n s) -> p n s", s=256),
                                                     axis=AX.X, op=ALU.add), reads=[bkt], writes=[bks])
                        dve.op(V.tensor_copy, dict(out=kshi[:], in_=ks[:]), reads=[bks], writes=[bks])
                        dve.op(V.tensor_tensor, dict(out=kslo[:], in0=ks[:], in1=kshi[:], op=ALU.subtract),
                               reads=[bks], writes=[bks])
                        for i in range(16):
                            pe.op(T.matmul, dict(out=MBk[:, i * 16:(i + 1) * 16], lhsT=qt[:, i * 128:(i + 1) * 128],
                                                 rhs=kshi[:], start=True, stop=False),
                                  reads=[bqt, bks], writes=[BMBk], track=False)
                            pe.op(T.matmul, dict(out=MBk[:, i * 16:(i + 1) * 16], lhsT=qt[:, i * 128:(i + 1) * 128],
                                                 rhs=kslo[:], start=False, stop=True),
                                  reads=[bqt, bks], writes=[BMBk], track=(i == 15))
                        dve.op(V.tensor_tensor, dict(out=gsb[:], in0=MBk[:, 0:256], in1=gm[:].rearrange("p i n -> p (i n)"),
                                                     op=ALU.add), reads=[BMBk, bsetup], writes=[bg])
                        for i in range(16):
                            dve.op(V.max, dict(out=top8[:, i * 8:(i + 1) * 8], in_=gsb[:, i * 16:(i + 1) * 16]),
                                   reads=[bg], writes=[bg])
                            dve.op(V.tensor_scalar, dict(out=mb[:, i * 16:(i + 1) * 16], in0=gsb[:, i * 16:(i + 1) * 16],
                                                         scalar1=top8[:, i * 8 + 2:i * 8 + 3], scalar2=NEG,
                                                         op0=ALU.is_lt, op1=ALU.mult), reads=[bg], writes=[bg])
                        dve.op(V.tensor_scalar, dict(out=mb2[:], in0=gsb[:], scalar1=-20000.0, scalar2=NEG,
                                                     op0=ALU.is_lt, op1=ALU.mult), reads=[bg], writes=[bg])
                        dve.op(V.tensor_tensor, dict(out=mb[:], in0=mb[:], in1=mb2[:], op=ALU.add), reads=[bg], writes=[bg])
                        for i in range(16):
                            nb = i * 16 + 8 + i // 2
                            dve.op(V.memset, dict(ap=mb[:, nb:nb + 1], constant=0.0), reads=[bg], writes=[bg])
                    else:
                        for j in range(32):
                            if j < 16:
                                dve.op(V.tensor_scalar, dict(out=fb[p][:, j, :], in0=crefb[:, hh, :], scalar1=ccolT[:, j, hh:hh + 1],
                                                              scalar2=ctxm[:, 0:1], op0=ALU.subtract, op1=ALU.add),
                                       reads=[bsetup, bsmall], writes=[bfb[p]])
                            else:
                                dve.op(V.tensor_scalar, dict(out=fb[p][:, j, :], in0=crefb[:, hh, :], scalar1=ccolT[:, j, hh:hh + 1],
                                                              scalar2=None, op0=ALU.subtract),
                                       reads=[bsetup], writes=[bfb[p]])

                def prepB(hi):
                    hd = heads[hi]
                    if hd >= 16:
                        return
                    p = hi % 2
                    MBk, BMBk = MBs[p]
                    for ib in range(4):
                        for ii in range(4):
                            i = ib * 4 + ii
                            pe.op(T.matmul, dict(out=MBk[0:16, ii * 128:(ii + 1) * 128], lhsT=mb[:, i * 16:(i + 1) * 16],
                                                 rhs=identf[:], start=True, stop=True),
                                  reads=[bg, bconst], writes=[BMBk], track=(ii == 3))
                        dve.op(V.tensor_copy, dict(out=MT[p][0:16, ib * 512:(ib + 1) * 512], in_=MBk[0:16, :]),
                               reads=[BMBk], writes=[bMT[p]])

                steps = []
                for c in range(4):
                    last = 16 + 4 * c + 3
                    for j in range(last + 1):
                        steps.append((c, j, last))
                NSTEP = len(steps)
                HOOK_A = 2
                HOOK_B = 20
                LA = 3
                gsi = [0]

                def sweep(hi):
                    hd = heads[hi]
                    moba = hd < 16
                    hh = hd % 16
                    kt, vt, qt, zt, bkt, bvt, bqt, bzt = slot_of(hi)
                    p = hi % 2
                    ys = hi % 2
                    base = gsi[0]

                    def qk(si):
                        c, j, last = steps[si]
                        r0 = max(0, j - (16 + 4 * c))
                        lo = r0 * 128
                        ring = (base + si) % 4
                        sbank, bsb = SB[ring], BSB[ring]
                        q0 = c * 512 + lo
                        pe.op(T.matmul, dict(out=sbank[:, lo:512], lhsT=kt[:, j * 128:(j + 1) * 128],
                                             rhs=qt[:, q0:(c + 1) * 512], start=True, stop=(not moba)),
                              reads=[bkt, bqt], writes=[bsb], track=(not moba))
                        if moba:
                            pe.op(T.matmul, dict(out=sbank[:, lo:512], lhsT=selb[:, j // 2, :],
                                                 rhs=MT[p][:, q0:(c + 1) * 512], start=False, stop=True),
                                  reads=[bsetup, bMT[p]], writes=[bsb])
                            if j >= 16 + 4 * c:
                                dve.op(V.tensor_tensor, dict(out=sbank[:, lo:lo + 128], in0=sbank[:, lo:lo + 128],
                                                             in1=tbx[:, 2 * hh, :], op=ALU.add),
                                       reads=[bsb, btbx], writes=[bsb])
                                if r0 + 1 < 4:
                                    dve.op(V.tensor_tensor, dict(out=sbank[:, lo + 128:lo + 256], in0=sbank[:, lo + 128:lo + 256],
                                                                 in1=tbx[:, 2 * hh + 1, :], op=ALU.add),
                                           reads=[bsb, btbx], writes=[bsb])
                            elif j == 16 + 4 * c - 1:
                                dve.op(V.tensor_tensor, dict(out=sbank[:, 0:128], in0=sbank[:, 0:128],
                                                             in1=tbx[:, 2 * hh + 1, :], op=ALU.add),
                                       reads=[bsb, btbx], writes=[bsb])
                            bias_ap = b31b[:, hh:hh + 1]
                            rb = [bsmall]
                        else:
                            if j >= 16 + 4 * c:
                                dve.op(V.tensor_tensor, dict(out=sbank[:, lo:lo + 128], in0=sbank[:, lo:lo + 128],
                                                             in1=cm[:], op=ALU.add),
                                       reads=[bsb, bsetup], writes=[bsb])
                            bias_ap = fb[p][:, j, c:c + 1]
                            rb = [bfb[p]]
                        pring = (base + si) % NPT
                        act.op(A.activation, dict(out=PT[pring][:, lo:512], in_=sbank[:, lo:512], func=AF.Exp,
                                                  bias=bias_ap, scale=1.0),
                               reads=[bsb] + rb, writes=[bPT[pring]])

                    def pv(si):
                        c, j, last = steps[si]
                        r0 = max(0, j - (16 + 4 * c))
                        lo = r0 * 128
                        pring = (base + si) % NPT
                        py, bpy = PY[(hi * 4 + c) % 2]
                        psm, bpsm = PSM
                        pe.op(T.matmul, dict(out=py[:, lo:512], lhsT=Vtok[p][:, j, :], rhs=PT[pring][:, lo:512],
                                             start=(j == 0), stop=(j == last)),
                              reads=[bVtok[p][j // 8], bPT[pring]], writes=[bpy], track=True)
                        DEFER = 3
                        if j >= DEFER:
                            todo = list(range(0, DEFER + 1)) if j == DEFER else [j]
                            for jj in todo:
                                prj = (base + si - (j - jj)) % NPT
                                pe.op(T.matmul, dict(out=psm[:, 0:512] if jj < DEFER + 1 and j == DEFER else psm[:, lo:512],
                                                     lhsT=onesb[:],
                                                     rhs=PT[prj][:, 0:512] if jj < DEFER + 1 and j == DEFER else PT[prj][:, lo:512],
                                                     start=(jj == 0), stop=(jj == last)),
                                      reads=[bsetup, bPT[prj]], writes=[bpsm], track=True)
                        if j == last:
                            dve.op(V.reciprocal, dict(out=rc[:], in_=psm[:]), reads=[bpsm], writes=[brc])
                            dve.op(V.tensor_tensor, dict(out=yt[:], in0=py[:], in1=rc[:], op=ALU.mult),
                                   reads=[bpy, brc], writes=[byt])
                            dve.op(V.tensor_tensor, dict(out=ygst[ys][:, c * 512:(c + 1) * 512], in0=yt[:],
                                                         in1=zt[:, c * 512:(c + 1) * 512], op=ALU.mult),
                                   reads=[byt, bzt], writes=[bygst[ys]])

                    for si in range(NSTEP + LA):
                        if si == HOOK_A and hi + 1 < NH:
                            prepA(hi + 1)
                        if si == HOOK_B and hi + 1 < NH:
                            prepB(hi + 1)
                        if si < NSTEP:
                            qk(si)
                        if si - LA >= 0:
                            pv(si - LA)
                    gsi[0] += NSTEP
                    yrow = hd * 128
                    sp.dma(YG[yrow:yrow + 128, :], ygst[ys][:], reads=[bygst[ys]])

                if NH:
                    prepA(0)
                    prepB(0)
                setup_tbx()
                nconv = 0
                for hi in range(NH):
                    if hi + 2 < NH:
                        load_head(heads[hi + 2], (hi + 2) % 3)
                    if hi >= 2 or NH < 8:
                        pool._wait(pe.sem, pe.n)
                        for _ in range(3 if NH >= 8 else 16):
                            convert_weights(nconv)
                            nconv += 1
                    sweep(hi)
                while nconv < 64:
                    convert_weights(nconv)
                    nconv += 1
                fw.barrier()
                fw.stack = st

        if 3 in phases:
            fw.barrier()
            with ExitStack() as p3:
                fw.stack = p3
                T3 = 256
                ygT = fw.sb("ygT", [128, 32, T3], BF16)
                bygT = fw.buf("ygT", dma=True)
                mT = fw.sb("mT", [128, 32, T3], BF16)
                bmT = [fw.buf(f"mT{i}") for i in range(8)]
                rr = [fw.sb(f"rr{i}", [128, D], F32) for i in range(2)]
                brr = [fw.buf(f"rr{i}", dma=True) for i in range(2)]
                gain = fw.sb("gain", [128, D], F32)
                bias = fw.sb("bias", [128, D], F32)
                bgb = fw.buf("gb", dma=True)
                wslot = [fw.sb(f"ws{i}", [128, 32, 512], BF16) for i in range(2)]
                bws = [fw.buf(f"ws{i}", dma=True) for i in range(2)]
                gt = [fw.sb(f"gt{i}", [128, 8, T3], F32) for i in range(2)]
                bgt = [fw.buf(f"gt{i}", dma=True) for i in range(2)]
                t1 = fw.sb("t1", [128, T3], F32)
                t2 = fw.sb("t2", [128, T3], F32)
                bt1 = fw.buf("t1")
                bt2 = fw.buf("t2")
                stt = fw.sb("stt", [128, 8, 6], F32)
                mv = fw.sb("mv", [128, 4], F32)
                bst = fw.buf("st")
                epsb = fw.sb("epsb", [128, 1], F32)
                dve.op(V.memset, dict(ap=epsb[:], constant=EPS), writes=[bgb])
                sp.dma(gain[:], lng, writes=[bgb])
                sp.dma(bias[:], lnb, writes=[bgb])
                NT3 = NOWN // T3
                jobs = []
                for tt in range(NT3):
                    for ct in range(8):
                        jobs.append(("br", tt, ct))
                    for ct in range(8):
                        jobs.append(("out", tt, ct))

                def load_w(n):
                    kind, tt, ct = jobs[n]
                    slot = n % 2
                    srcw = WBR16 if kind == "br" else WO16
                    sp.dma(wslot[slot][:], srcw[:, ct * 512:(ct + 1) * 512].rearrange("(w p) c -> p w c", p=128),
                           writes=[bws[slot]])

                def load_gates(n):
                    kind, tt, ct = jobs[n]
                    if kind != "br":
                        return
                    gs = (tt * 8 + ct) % 2
                    tok0 = tt * T3
                    for br in range(2):
                        r0 = br * 4096 + ct * 512
                        kw = dict(writes=[bgt[gs]]) if br == 0 else dict(free_owner=bgt[gs])
                        sp.dma(gt[gs][:, br * 4:(br + 1) * 4, :],
                               SG[r0:r0 + 512, tok0:tok0 + T3].rearrange("(g p) t -> p g t", p=128), **kw)

                def load_yg(tt):
                    tok0 = tt * T3
                    act.dma(ygT[:], YG[:, tok0:tok0 + T3].rearrange("(w p) t -> p w t", p=128), writes=[bygT])

                load_yg(0)
                load_w(0)
                load_gates(0)
                for n, (kind, tt, ct) in enumerate(jobs):
                    tok0 = tt * T3
                    slot = n % 2
                    if n + 1 < len(jobs):
                        load_w(n + 1)
                        load_gates(n + 1)
                    if kind == "br":
                        gs = (tt * 8 + ct) % 2
                        for g in range(4):
                            dc = ct * 4 + g
                            ba, bba = nextbank()
                            for wc in range(16):
                                pe.op(T.matmul, dict(out=ba[:, 0:T3], lhsT=wslot[slot][:, wc, g * 128:(g + 1) * 128],
                                                     rhs=ygT[:, wc, :], start=(wc == 0), stop=(wc == 15)),
                                      reads=[bws[slot], bygT], writes=[bba], track=(wc == 15))
                            bf_, bbf = nextbank()
                            for wc in range(16):
                                pe.op(T.matmul, dict(out=bf_[:, 0:T3], lhsT=wslot[slot][:, 16 + wc, g * 128:(g + 1) * 128],
                                                     rhs=ygT[:, 16 + wc, :], start=(wc == 0), stop=(wc == 15)),
                                      reads=[bws[slot], bygT], writes=[bbf], track=(wc == 15))
                            dve.op(V.tensor_tensor, dict(out=t1[:], in0=ba[:, 0:T3], in1=gt[gs][:, g, :], op=ALU.mult),
                                   reads=[bba, bgt[gs]], writes=[bt1])
                            dve.op(V.tensor_tensor, dict(out=t2[:], in0=bf_[:, 0:T3], in1=gt[gs][:, 4 + g, :], op=ALU.mult),
                                   reads=[bbf, bgt[gs]], writes=[bt2])
                            dve.op(V.tensor_tensor, dict(out=mT[:, dc, :], in0=t1[:], in1=t2[:], op=ALU.add),
                                   reads=[bt1, bt2], writes=[bmT[ct]])
                        continue
                    if ct == 0:
                        if tt + 1 < NT3:
                            load_yg(tt + 1)
                        for ts in range(2):
                            sp.dma(rr[ts][:], xo[tok0 + ts * 128: tok0 + (ts + 1) * 128, :], writes=[brr[ts]])
                    for ts in range(2):
                        bk, bbk = nextbank()
                        for kc in range(32):
                            pe.op(T.matmul, dict(out=bk[:], lhsT=mT[:, kc, ts * 128:(ts + 1) * 128],
                                                 rhs=wslot[slot][:, kc, :], start=(kc == 0), stop=(kc == 31)),
                                  reads=[bws[slot], bmT[kc // 4]], writes=[bbk], track=(kc == 31))
                        dve.op(V.scalar_tensor_tensor, dict(out=rr[ts][:, ct * 512:(ct + 1) * 512],
                                                            in0=rr[ts][:, ct * 512:(ct + 1) * 512], scalar=ALPHA,
                                                            in1=bk[:], op0=ALU.mult, op1=ALU.add),
                               reads=[bbk, brr[ts]], writes=[brr[ts]])
                    if ct != 7:
                        continue
                    for ts in range(2):
                        for q in range(8):
                            dve.op(V.bn_stats, dict(out=stt[:, q, :], in_=rr[ts][:, q * 512:(q + 1) * 512]),
                                   reads=[brr[ts]], writes=[bst])
                        dve.op(V.bn_aggr, dict(out=mv[:, 0:2], in_=stt[:].rearrange("p q s -> p (q s)")),
                               reads=[bst], writes=[bst])
                        act.op(A.activation, dict(out=mv[:, 2:3], in_=mv[:, 1:2], func=AF.Sqrt, bias=epsb[:, 0:1], scale=1.0),
                               reads=[bst, bgb], writes=[bst])
                        dve.op(V.reciprocal, dict(out=mv[:, 2:3], in_=mv[:, 2:3]), reads=[bst], writes=[bst])
                        dve.op(V.tensor_scalar, dict(out=rr[ts][:], in0=rr[ts][:], scalar1=mv[:, 0:1], scalar2=mv[:, 2:3],
                                                     op0=ALU.subtract, op1=ALU.mult), reads=[bst, brr[ts]], writes=[brr[ts]])
                        dve.op(V.tensor_tensor, dict(out=rr[ts][:], in0=rr[ts][:], in1=gain[:], op=ALU.mult),
                               reads=[brr[ts], bgb], writes=[brr[ts]])
                        dve.op(V.tensor_tensor, dict(out=rr[ts][:], in0=rr[ts][:], in1=bias[:], op=ALU.add),
                               reads=[brr[ts], bgb], writes=[brr[ts]])
                        pool.dma(out[tok0 + ts * 128: tok0 + (ts + 1) * 128, :], rr[ts][:], reads=[brr[ts]])
                fw.barrier()
                fw.stack = st
        else:
            pass

        fw.barrier()
        fw.replay()
    return nc


def _t5_bucket(d):
    d = np.asarray(d, np.int64)
    df = np.maximum(d, 1).astype(np.float32)
    large = 16 + (np.log(df / np.float32(16)) / np.float32(np.log(128 / 16)) * np.float32(16)).astype(np.int32)
    large = np.minimum(large, 31)
    return np.where(d < 16, d, large)


def make_in_maps(x, w_in, b_forget, b_gate, rel_bias_table, w_branch, w_out, ln_gain, ln_bias, cores=range(8)):
    f32 = np.float32
    x = np.asarray(x, f32)
    w_in2 = np.ascontiguousarray(np.asarray(w_in, f32)[0])
    w_br = np.ascontiguousarray(np.asarray(w_branch, f32)[0])
    w_o = np.ascontiguousarray(np.asarray(w_out, f32)[0])
    table = np.asarray(rel_bias_table, f32)
    s = np.arange(128)[:, None]
    t = np.arange(128)[None, :]
    bd = _t5_bucket(np.maximum(t - s, 0))
    bo = _t5_bucket(128 + t - s)
    tb = np.stack([np.stack([table[bd, h], table[bo, h]], 0) for h in range(16)], 0).astype(f32)
    b31 = np.ascontiguousarray(np.broadcast_to(table[31][None, :], (128, 16))).astype(f32)
    bfT = np.ascontiguousarray(np.asarray(b_forget, f32)[0].reshape(16, 1))
    bgT = np.ascontiguousarray(np.asarray(b_gate, f32)[0].reshape(64, 128).T)
    lng = np.ascontiguousarray(np.broadcast_to(np.asarray(ln_gain, f32)[0][None, :], (128, D)))
    lnb = np.ascontiguousarray(np.broadcast_to(np.asarray(ln_bias, f32)[0][None, :], (128, D)))
    maps = []
    for c in cores:
        b, h = c // 2, c % 2
        xo = np.ascontiguousarray(x[b, h * NOWN:(h + 1) * NOWN])
        xc = np.ascontiguousarray(x[b, 0:NCTX] if h == 1 else x[b, NOWN:NOWN + NCTX])
        ctxm = np.full((128, 1), 0.0 if h == 1 else NEG, f32)
        maps.append({"xo": xo, "xc": xc, "w_in": w_in2, "bfT": bfT, "bgT": bgT, "tb": tb, "b31": b31,
                     "w_br": w_br, "w_out": w_o, "lng": lng, "lnb": lnb, "ctxm": ctxm})
    return maps


def kernel(x, w_in, b_forget, b_gate, rel_bias_table, w_branch, w_out, ln_gain, ln_bias):
    maps = make_in_maps(x, w_in, b_forget, b_gate, rel_bias_table, w_branch, w_out, ln_gain, ln_bias)
    nc = build_program()
    res = run_bass_kernel_spmd(nc, maps, core_ids=list(range(8)))
    outp = np.empty((4, 4096, D), np.float32)
    for c in range(8):
        b, h = c // 2, c % 2
        outp[b, h * NOWN:(h + 1) * NOWN] = res.results[c]["out"]
    return outp
```

```python
from contextlib import ExitStack
from concourse.bass_utils import run_bass_kernel_spmd
import numpy as np
import concourse.bass as bass
import concourse.mybir as mybir

F32 = mybir.dt.float32
BF16 = mybir.dt.bfloat16
AF = mybir.ActivationFunctionType
ALU = mybir.AluOpType
AX = mybir.AxisListType


class Sem:
    _n = 0

    def __init__(self, h):
        self.h = h
        Sem._n += 1
        self.id = Sem._n


class Buf:
    def __init__(self, name, dsem=None):
        self.name = name
        self.ew = None
        self.er = {}
        self.dsem = dsem
        self.dcnt = 0


class Eng:
    def __init__(self, fw, name, eng, sem, same_sync):
        self.fw = fw
        self.name = name
        self.e = eng
        self.sem = sem
        self.n = 0
        self.waited = {}
        self.same_sync = same_sync
        self.thunks = []

    def _wait(self, sem, val):
        if val <= 0:
            return
        if self.waited.get(sem.id, 0) >= val:
            return
        self.waited[sem.id] = val
        e = self.e
        h = sem.h
        self.thunks.append(lambda: e.wait_ge(h, val))

    def _dep_eng(self, w):
        if w is None:
            return
        eng, n = w
        if eng is self and not self.same_sync:
            return
        self._wait(eng.sem, n)

    def deps(self, reads, writes):
        for b in reads:
            self._dep_eng(b.ew)
            if b.dsem is not None and b.dcnt:
                self._wait(b.dsem, 16 * b.dcnt)
        for b in writes:
            self._dep_eng(b.ew)
            for eng, n in b.er.items():
                self._dep_eng((eng, n))
            if b.dsem is not None and b.dcnt:
                self._wait(b.dsem, 16 * b.dcnt)

    def op(self, method, kw, reads=(), writes=(), track=True):
        self.deps(reads, writes)
        if track:
            self.n += 1
            n = self.n
            h = self.sem.h
            self.thunks.append(lambda: method(**kw).then_inc(h, 1))
        else:
            n = self.n + 1
            self.thunks.append(lambda: method(**kw))
        for b in reads:
            if b.er.get(self, 0) < n:
                b.er[self] = n
        for b in writes:
            b.ew = (self, n)
            b.er = {}

    def dma(self, out, in_, reads=(), writes=(), free_owner=None, **kw):
        self.deps(reads, writes)
        owner = free_owner
        for b in list(writes) + list(reads):
            if owner is None and b.dsem is not None:
                owner = b
                break
        assert owner is not None
        owner.dcnt += 1
        h = owner.dsem.h
        e = self.e
        self.thunks.append(lambda: e.dma_start(out=out, in_=in_, **kw).then_inc(h, 16))
        for b in writes:
            if b is not owner:
                raise AssertionError("multi-buffer dma write")
            b.ew = None
            b.er = {}

    def wait_dma(self, b):
        if b.dsem is not None and b.dcnt:
            self._wait(b.dsem, 16 * b.dcnt)


class FW:
    def __init__(self, nc, stack, same_sync=True):
        self.nc = nc
        self.stack = stack
        self.semstack = stack
        self.sems = []
        self.dbufs = []

        def mk(name, eng, ss):
            return Eng(self, name, eng, self.sem(name), ss)
        self.pe = mk("pe", nc.tensor, False)
        self.act = mk("act", nc.scalar, same_sync)
        self.dve = mk("dve", nc.vector, same_sync)
        self.pool = mk("pool", nc.gpsimd, same_sync)
        self.sp = mk("sp", nc.sync, False)
        self.engs = [self.pe, self.act, self.dve, self.pool, self.sp]

    def sem(self, name):
        h = self.semstack.enter_context(self.nc.semaphore(name))
        s = Sem(h)
        self.sems.append(s)
        return s

    def sb(self, name, shape, dt):
        return self.stack.enter_context(self.nc.sbuf_tensor("s_" + name, shape, dt))

    def ps(self, name, shape, dt):
        return self.stack.enter_context(self.nc.psum_tensor(name, shape, dt))

    def buf(self, name, dma=False):
        b = Buf(name, self.sem("d_" + name) if dma else None)
        if dma:
            self.dbufs.append(b)
        return b

    def barrier(self):
        for e in self.engs:
            for o in self.engs:
                if o is not e and o.n:
                    e._wait(o.sem, o.n)
            for b in self.dbufs:
                e.wait_dma(b)

    def replay(self):
        nc = self.nc
        with nc.Block() as block:
            @block.tensor
            def _(e):
                for t in self.pe.thunks:
                    t()

            @block.scalar
            def _(e):
                for t in self.act.thunks:
                    t()

            @block.vector
            def _(e):
                for t in self.dve.thunks:
                    t()

            @block.gpsimd
            def _(e):
                for t in self.pool.thunks:
                    t()

            @block.sync
            def _(e):
                for t in self.sp.thunks:
                    t()


D = 4096
NOWN = 2048
NCTX = 2048
NTOK = NOWN + NCTX
NCOL = 24592
GCOL0 = 16400
SCALE = 128 ** -0.5
ALPHA = 2.0 ** 0.25
EPS = 1e-5
NEG = -30000.0


def build_program(debug=False, phases=(1, 2, 3), debug_heads=(0, 16), p2stop=9):
    nc = bass.Bass("TRN2", target_bir_lowering=False)

    def din(name, shape, dt=F32):
        return nc.dram_tensor(name, shape, dt, kind="ExternalInput").ap()

    def dscr(name, shape, dt):
        kind = "ExternalOutput" if debug else "Internal"
        return nc.dram_tensor(name, shape, dt, kind=kind).ap()

    xo = din("xo", [NOWN, D])
    xc = din("xc", [NCTX, D])
    w_in = din("w_in", [D if 1 in phases else 1, NCOL])
    nbfT = din("bfT", [16, 1])
    bgT = din("bgT", [128, 64])
    tb = din("tb", [16, 2, 128, 128])
    b31 = din("b31", [128, 16])
    w_br = din("w_br", [2, 2048 if 3 in phases else 1, D])
    w_out = din("w_out", [D if 3 in phases else 1, D])
    lng = din("lng", [128, D])
    lnb = din("lnb", [128, D])
    ctxm_d = din("ctxm", [128, 1])
    out = nc.dram_tensor("out", [NOWN, D], F32, kind="ExternalOutput").ap()

    S1 = dscr("S1", [16384, NOWN], BF16)
    SC = dscr("SC", [8192, NCTX], BF16)
    SG = dscr("SG", [8192, NOWN], F32)
    YG = dscr("YG", [4096, NOWN], BF16)
    WBR16 = nc.dram_tensor("WBR16", [4096, D], BF16, kind="Internal").ap()
    WO16 = nc.dram_tensor("WO16", [D, D], BF16, kind="Internal").ap()
    CTd = dscr("CTd", [16, NTOK], F32) if debug else None

    with ExitStack() as st:
        fw = FW(nc, st)
        pe, act, dve, pool, sp = fw.pe, fw.act, fw.dve, fw.pool, fw.sp
        T, V, A, G = nc.tensor, nc.vector, nc.scalar, nc.gpsimd

        banks = [fw.ps(f"bank{i}", [128, 512], F32) for i in range(8)]
        bbanks = [fw.buf(f"bank{i}") for i in range(8)]
        bankctr = [0]

        def nextbank():
            i = bankctr[0] % 8
            bankctr[0] += 1
            return banks[i], bbanks[i]

        identf = fw.sb("identf", [128, 128], F32)
        identb = fw.sb("identb", [128, 128], BF16)
        bconst = fw.buf("const")
        cT = fw.sb("cT", [16, NTOK], F32)
        bcT = fw.buf("cT", dma=True)
        nbf = fw.sb("nbf", [16, 1], F32)
        bsmall = fw.buf("small", dma=True)
        bgs = fw.sb("bgs", [128, 64], F32)
        b31b = fw.sb("b31b", [128, 16], F32)
        ctxm = fw.sb("ctxm", [128, 1], F32)

        bwconv = fw.buf("wconv", dma=True)
        sp.dma(nbf[:], nbfT, writes=[bsmall])
        sp.dma(bgs[:], bgT, writes=[bsmall])
        sp.dma(b31b[:], b31, writes=[bsmall])
        sp.dma(ctxm[:], ctxm_d, writes=[bsmall])
        dve.op(V.tensor_scalar, dict(out=nbf[:], in0=nbf[:], scalar1=-1.0, scalar2=None, op0=ALU.mult),
               reads=[bsmall], writes=[bsmall])
        pool.op(G.memset, dict(ap=identf[:], constant=1.0), writes=[bconst])
        pool.op(G.affine_select, dict(out=identf[:], in_=identf[:], pattern=[[-1, 128]],
                                      compare_op=ALU.is_equal, fill=0.0, base=0, channel_multiplier=1),
                reads=[bconst], writes=[bconst])
        dve.op(V.tensor_copy, dict(out=identb[:], in_=identf[:]), reads=[bconst], writes=[bconst])

        if 1 in phases:
            with ExitStack() as p1:
                fw.stack = p1
                TT = 1024
                xT = fw.sb("xT", [128, 32, TT], BF16)
                bxT = [[fw.buf(f"xT{ts}_{kb}") for kb in range(4)] for ts in range(8)]
                wring = [fw.sb(f"wr{i}", [128, 32, 512], BF16) for i in range(2)]
                bw = [fw.buf(f"wr{i}", dma=True) for i in range(2)]
                xld = [fw.sb(f"xld{i}", [128, D], BF16) for i in range(3)]
                bxld = [fw.buf(f"xld{i}", dma=True) for i in range(3)]
                stgb = [fw.sb(f"stgb{i}", [128, 4, 512], BF16) for i in range(2)]
                bstgb = [fw.buf(f"stgb{i}", dma=True) for i in range(2)]
                stgf = [fw.sb(f"stgf{i}", [128, 4, 512], F32) for i in range(2)]
                bstgf = [fw.buf(f"stgf{i}", dma=True) for i in range(2)]
                wf = fw.sb("wf", [128, 32, 16], BF16)
                bwf = fw.buf("wf", dma=True)
                lft = fw.sb("lft", [16, TT], F32)
                blft = fw.buf("lft")
                ones16 = fw.sb("ones16", [16, TT], F32)
                bones16 = fw.buf("ones16")
                dve.op(V.memset, dict(ap=ones16[:], constant=1.0), writes=[bones16])
                pool.dma(wf[:], w_in[:, 16384:16400].rearrange("(k p) c -> p k c", p=128), writes=[bwf])

                own_tiles = []
                kinds = ["q", "k", "v", "z", "q", "k", "v", "z"]
                for sec in range(8):
                    for i in range(4):
                        c0 = sec * 2048 + i * 512
                        own_tiles.append((c0, kinds[sec], S1, c0))
                for i in range(16):
                    own_tiles.append((GCOL0 + i * 512, "g", SG, i * 512))
                ctx_tiles = []
                for wi, sec in enumerate([1, 2, 5, 6]):
                    for i in range(4):
                        c0 = sec * 2048 + i * 512
                        ctx_tiles.append((c0, "k", SC, wi * 2048 + i * 512))

                wcount = 0
                scount = [0, 0]
                evtog = 0
                for tt in range(4):
                    is_ctx = tt < 2
                    src = xc if is_ctx else xo
                    row0 = (tt % 2) * TT
                    gtok0 = tt * TT
                    for ts in range(8):
                        b = ts % 3
                        pool.dma(xld[b][:], src[row0 + ts * 128: row0 + (ts + 1) * 128, :], writes=[bxld[b]])
                        for kb in range(4):
                            bank, bb = nextbank()
                            bankb = bank[:].bitcast(BF16)
                            for kk in range(8):
                                k = kb * 8 + kk
                                pe.op(T.transpose, dict(out=bankb[:, kk * 128:(kk + 1) * 128],
                                                        in_=xld[b][:, k * 128:(k + 1) * 128], identity=identb[:]),
                                      reads=[bxld[b], bconst], writes=[bb], track=(kk == 7))
                            eng, meth = (act, A.copy) if (evtog % 2 == 0) else (dve, V.tensor_copy)
                            evtog += 1
                            eng.op(meth, dict(out=xT[:, kb * 8:(kb + 1) * 8, ts * 128:(ts + 1) * 128],
                                              in_=bankb.rearrange("p (k t) -> p k t", k=8)),
                                   reads=[bb], writes=[bxT[ts][kb]])
                    for c in range(2):
                        bank, bb = nextbank()
                        for k in range(32):
                            pe.op(T.matmul, dict(out=bank[0:16, :], lhsT=wf[:, k, :], rhs=xT[:, k, c * 512:(c + 1) * 512],
                                                 start=(k == 0), stop=(k == 31)),
                                  reads=[bwf] + [bxT[c * 4 + t4][k // 8] for t4 in range(4)], writes=[bb], track=(k == 31))
                        act.op(A.activation, dict(out=lft[:, c * 512:(c + 1) * 512], in_=bank[0:16, :], func=AF.Exp,
                                                  bias=nbf[:, 0:1], scale=-1.0),
                               reads=[bb, bsmall], writes=[blft])
                    act.op(A.activation, dict(out=lft[:], in_=lft[:], func=AF.Ln, bias=1.0, scale=1.0),
                           reads=[blft], writes=[blft])
                    init = 0.0 if tt == 0 else cT[:, gtok0 - 1:gtok0]
                    dve.op(V.tensor_tensor_scan, dict(out=cT[:, gtok0:gtok0 + TT], data0=ones16[:], data1=lft[:],
                                                      initial=init, op0=ALU.mult, op1=ALU.subtract),
                           reads=[blft, bones16, bcT], writes=[bcT])
                    tiles = ctx_tiles if is_ctx else own_tiles
                    for (c0, kind, dst, drow0) in tiles:
                        slot = wcount % 2
                        wcount += 1
                        pool.dma(wring[slot][:], w_in[:, c0:c0 + 512].rearrange("(k p) c -> p k c", p=128),
                                 writes=[bw[slot]])
                        for c in range(2):
                            isg = kind == "g"
                            si = 1 if isg else 0
                            ss = scount[si] % 2
                            scount[si] += 1
                            stg, bstg = (stgf[ss], bstgf[ss]) if isg else (stgb[ss], bstgb[ss])
                            for g in range(4):
                                bank, bb = nextbank()
                                for k in range(32):
                                    pe.op(T.matmul, dict(out=bank[:], lhsT=wring[slot][:, k, g * 128:(g + 1) * 128],
                                                         rhs=xT[:, k, c * 512:(c + 1) * 512],
                                                         start=(k == 0), stop=(k == 31)),
                                          reads=[bw[slot]] + [bxT[c * 4 + t4][k // 8] for t4 in range(4)],
                                          writes=[bb], track=(k == 31))
                                if kind == "q":
                                    act.op(A.activation, dict(out=stg[:, g, :], in_=bank[:], func=AF.Copy, scale=SCALE),
                                           reads=[bb], writes=[bstg])
                                elif kind == "z":
                                    act.op(A.activation, dict(out=stg[:, g, :], in_=bank[:], func=AF.Silu),
                                           reads=[bb], writes=[bstg])
                                elif kind == "g":
                                    gi = (c0 - GCOL0) // 128 + g
                                    act.op(A.activation, dict(out=stg[:, g, :], in_=bank[:], func=AF.Sigmoid,
                                                              bias=bgs[:, gi:gi + 1], scale=1.0),
                                           reads=[bb, bsmall], writes=[bstg])
                                else:
                                    dve.op(V.tensor_copy, dict(out=stg[:, g, :], in_=bank[:]), reads=[bb], writes=[bstg])
                            t0 = row0 + c * 512
                            sp.dma(dst[drow0:drow0 + 512, t0:t0 + 512].rearrange("(g p) t -> p g t", p=128), stg[:],
                                   reads=[bstg])
                if debug:
                    sp.dma(CTd, cT[:], reads=[bcT])
                fw.barrier()
                fw.stack = st

        if 2 in phases:
            fw.barrier()
            with ExitStack() as p2:
                fw.stack = p2
                SB = [banks[0], banks[1], banks[2], banks[3]]
                BSB = [bbanks[0], bbanks[1], bbanks[2], bbanks[3]]
                PY = [(banks[4], bbanks[4]), (banks[5], bbanks[5])]
                PSM = (banks[6], bbanks[6])
                MBs = [(banks[7], bbanks[7]), (banks[7], bbanks[7])]
                MB, BMB = banks[7], bbanks[7]

                KT = [fw.sb(f"KT{i}", [128, NTOK], BF16) for i in range(3)]
                VT = [fw.sb(f"VT{i}", [128, NTOK], BF16) for i in range(3)]
                QT = [fw.sb(f"QT{i}", [128, NOWN], BF16) for i in range(3)]
                ZT = [fw.sb(f"ZT{i}", [128, NOWN], BF16) for i in range(3)]
                bKT = [fw.buf(f"KT{i}", dma=True) for i in range(3)]
                bVT = [fw.buf(f"VT{i}", dma=True) for i in range(3)]
                bQT = [fw.buf(f"QT{i}", dma=True) for i in range(3)]
                bZT = [fw.buf(f"ZT{i}", dma=True) for i in range(3)]
                Vtok = [fw.sb(f"Vtok{q}", [128, 32, 128], BF16) for q in range(2)]
                bVtok = [[fw.buf(f"Vtok{q}_{i}") for i in range(4)] for q in range(2)]
                NPT = 8
                PT = [fw.sb(f"PT{i}", [128, 512], BF16) for i in range(NPT)]
                bPT = [fw.buf(f"PT{i}") for i in range(NPT)]
                ygst = [fw.sb(f"ygst{i}", [128, NOWN], BF16) for i in range(2)]
                bygst = [fw.buf(f"ygst{i}", dma=True) for i in range(2)]
                rc = fw.sb("rc", [128, 512], F32)
                yt = fw.sb("yt", [128, 512], F32)
                brc = fw.buf("rc")
                byt = fw.buf("yt")
                onesb = fw.sb("onesb", [128, 128], BF16)
                onesf = fw.sb("onesf", [128, 128], F32)
                cm = fw.sb("cm", [128, 128], F32)
                tbx = fw.sb("tbx", [128, 32, 128], F32)
                tbxh = fw.sb("tbxh", [128, 32, 128], BF16)
                tbxl = fw.sb("tbxl", [128, 32, 128], BF16)
                cmb = fw.sb("cmb", [128, 128], BF16)
                btbx = fw.buf("tbx", dma=True)
                self_f = fw.sb("self", [16, 16, 128], F32)
                selb = fw.sb("selb", [128, 16, 128], BF16)
                gm = fw.sb("gm", [128, 16, 16], F32)
                ccolT = fw.sb("ccolT", [128, 32, 16], F32)
                crefb = fw.sb("crefb", [128, 16, 4], F32)
                fb = [fw.sb(f"fb{q}", [128, 32, 4], F32) for q in range(2)]
                bfb = [fw.buf(f"fb{q}") for q in range(2)]
                bsetup = fw.buf("setup")
                ks = fw.sb("ks", [128, 16], F32)
                kshi = fw.sb("kshi", [128, 16], BF16)
                kslo = fw.sb("kslo", [128, 16], BF16)
                bks = fw.buf("ks")
                gsb = fw.sb("gsb", [128, 256], F32)
                top8 = fw.sb("top8", [128, 128], F32)
                mb = fw.sb("mb", [128, 256], F32)
                mb2 = fw.sb("mb2", [128, 256], F32)
                bg = fw.buf("gating")
                MT = [fw.sb(f"MT{q}", [128, NOWN], BF16) for q in range(2)]
                bMT = [fw.buf(f"MT{q}") for q in range(2)]

                dve.op(V.memset, dict(ap=onesb[:], constant=1.0), writes=[bsetup])
                dve.op(V.memset, dict(ap=onesf[:], constant=1.0), reads=[bsetup], writes=[bsetup])
                pool.op(G.memset, dict(ap=cm[:], constant=0.0), writes=[bsetup])
                pool.op(G.affine_select, dict(out=cm[:], in_=cm[:], pattern=[[1, 128]], compare_op=ALU.is_ge,
                                              fill=NEG, base=0, channel_multiplier=-1),
                        reads=[bsetup], writes=[bsetup])
                dve.op(V.tensor_copy, dict(out=cmb[:], in_=cm[:]), reads=[bsetup], writes=[bsetup])
                pool.op(G.memset, dict(ap=self_f[:], constant=1.0), writes=[bsetup])
                for n in range(16):
                    pool.op(G.affine_select, dict(out=self_f[:, n, :], in_=self_f[:, n, :], pattern=[[0, 128]],
                                                  compare_op=ALU.is_equal, fill=0.0, base=-n, channel_multiplier=1),
                            reads=[bsetup], writes=[bsetup])
                w_br_flat = w_br.rearrange("b w d -> (b w) d")

                def convert_weights(i):
                    if 3 not in phases or i >= 64:
                        return
                    if i < 32:
                        pool.dma(WBR16[i * 128:(i + 1) * 128, :], w_br_flat[i * 128:(i + 1) * 128, :], free_owner=bwconv)
                    else:
                        j = i - 32
                        pool.dma(WO16[j * 128:(j + 1) * 128, :], w_out[j * 128:(j + 1) * 128, :], free_owner=bwconv)
                dve.op(V.memset, dict(ap=selb[:], constant=0.0), reads=[bsetup], writes=[bsetup])
                dve.op(V.tensor_copy, dict(out=selb[0:16], in_=self_f[:]), reads=[bsetup], writes=[bsetup])
                for q in range(2):
                    dve.op(V.memset, dict(ap=MT[q][:], constant=0.0), writes=[bMT[q]])
                dve.op(V.memset, dict(ap=gm[:], constant=0.0), reads=[bsetup], writes=[bsetup])
                for i in range(16):
                    n0 = 8 + i // 2
                    dve.op(V.memset, dict(ap=gm[:, i, n0:16], constant=-1e30), reads=[bsetup], writes=[bsetup])
                for i in range(16):
                    dve.op(V.tensor_scalar, dict(out=gm[:, i, 0:8], in0=gm[:, i, 0:8], scalar1=ctxm[:, 0:1], scalar2=None,
                                                 op0=ALU.add), reads=[bsetup, bsmall], writes=[bsetup])
                def setup_tbx():
                    for h in range(16):
                        dve.op(V.tensor_scalar, dict(out=tbx[:, 2 * h:2 * h + 2, :], in0=tbx[:, 2 * h:2 * h + 2, :],
                                                     scalar1=b31b[:, h:h + 1], scalar2=None, op0=ALU.subtract),
                               reads=[btbx, bsmall], writes=[btbx])
                        dve.op(V.tensor_tensor, dict(out=tbx[:, 2 * h, :], in0=tbx[:, 2 * h, :], in1=cm[:], op=ALU.add),
                               reads=[btbx, bsetup], writes=[btbx])
                    dve.op(V.tensor_copy, dict(out=tbxh[:], in_=tbx[:]), reads=[btbx], writes=[btbx])
                    dve.op(V.tensor_tensor, dict(out=tbxl[:], in0=tbx[:], in1=tbxh[:], op=ALU.subtract),
                           reads=[btbx], writes=[btbx])
                for j in range(32):
                    pe.op(T.matmul, dict(out=MB[:, j * 16:(j + 1) * 16], lhsT=cT[:, j * 128:(j + 1) * 128],
                                         rhs=identf[0:16, 0:16], start=True, stop=True),
                          reads=[bcT, bconst], writes=[BMB], track=(j == 31))
                dve.op(V.tensor_copy, dict(out=ccolT[:].rearrange("p j h -> p (j h)"), in_=MB[:, 0:512]),
                       reads=[BMB], writes=[bsetup])
                for h in range(16):
                    pe.op(T.matmul, dict(out=MB[:, h * 4:(h + 1) * 4], lhsT=self_f[:, h, :],
                                         rhs=cT[:, NCTX:NTOK].rearrange("p (c s) -> p c s", s=512)[:, :, 0], start=True, stop=True),
                          reads=[bcT, bsetup], writes=[BMB], track=(h == 15))
                dve.op(V.tensor_copy, dict(out=crefb[:].rearrange("p h c -> p (h c)"), in_=MB[:, 0:64]),
                       reads=[BMB], writes=[bsetup])

                def load_head(hd, slot):
                    br, hh = hd // 16, hd % 16
                    base = br * 8192
                    qrow = base + hh * 128
                    krow = base + 2048 + hh * 128
                    vrow = base + 4096 + hh * 128
                    zrow = base + 6144 + hh * 128
                    ckrow = (2 * br) * 2048 + hh * 128
                    cvrow = (2 * br + 1) * 2048 + hh * 128
                    sp.dma(VT[slot][:, 0:NCTX], SC[cvrow:cvrow + 128, :], writes=[bVT[slot]])
                    sp.dma(VT[slot][:, NCTX:NTOK], S1[vrow:vrow + 128, :], free_owner=bVT[slot])
                    sp.dma(KT[slot][:, 0:NCTX], SC[ckrow:ckrow + 128, :], writes=[bKT[slot]])
                    sp.dma(KT[slot][:, NCTX:NTOK], S1[krow:krow + 128, :], free_owner=bKT[slot])
                    sp.dma(QT[slot][:], S1[qrow:qrow + 128, :], writes=[bQT[slot]])
                    sp.dma(ZT[slot][:], S1[zrow:zrow + 128, :], writes=[bZT[slot]])

                heads = list(range(32)) if not debug else list(debug_heads)
                if p2stop == 0:
                    heads = []
                NH = len(heads)
                if NH:
                    load_head(heads[0], 0)
                    if NH > 1:
                        load_head(heads[1], 1)
                sp.dma(tbx[:], tb.rearrange("h r s t -> s (h r) t"), writes=[btbx])

                def slot_of(hi):
                    s3 = hi % 3
                    return (KT[s3], VT[s3], QT[s3], ZT[s3], bKT[s3], bVT[s3], bQT[s3], bZT[s3])

                def prepA(hi):
                    hd = heads[hi]
                    moba = hd < 16
                    hh = hd % 16
                    kt, vt, qt, zt, bkt, bvt, bqt, bzt = slot_of(hi)
                    p = hi % 2
                    MBk, BMBk = MBs[p]
                    for jb in range(4):
                        mbb = MBk[:].bitcast(BF16)
                        for jj in range(8):
                            j = jb * 8 + jj
                            pe.op(T.transpose, dict(out=mbb[:, jj * 128:(jj + 1) * 128], in_=vt[:, j * 128:(j + 1) * 128],
                                                    identity=identb[:]),
                                  reads=[bvt, bconst], writes=[BMBk], track=(jj == 7))
                        dve.op(V.tensor_copy, dict(out=Vtok[p][:, jb * 8:(jb + 1) * 8, :],
                                                   in_=mbb.rearrange("p (j d) -> p j d", j=8)),
                               reads=[BMBk], writes=[bVtok[p][jb]])
                    if moba:
                        dve.op(V.tensor_reduce, dict(out=ks[:], in_=kt[:].rearrange("p (n s) -> p n s", s=256),
                                                     axis=AX.X, op=ALU.add), reads=[bkt], writes=[bks])
                        dve.op(V.tensor_copy, dict(out=kshi[:], in_=ks[:]), reads=[bks], writes=[bks])
                        dve.op(V.tensor_tensor, dict(out=kslo[:], in0=ks[:], in1=kshi[:], op=ALU.subtract),
                               reads=[bks], writes=[bks])
                        for i in range(16):
                            pe.op(T.matmul, dict(out=MBk[:, i * 16:(i + 1) * 16], lhsT=qt[:, i * 128:(i + 1) * 128],
                                                 rhs=kshi[:], start=True, stop=False),
                                  reads=[bqt, bks], writes=[BMBk], track=False)
                            pe.op(T.matmul, dict(out=MBk[:, i * 16:(i + 1) * 16], lhsT=qt[:, i * 128:(i + 1) * 128],
                                                 rhs=kslo[:], start=False, stop=True),
                                  reads=[bqt, bks], writes=[BMBk], track=(i == 15))
                        dve.op(V.tensor_tensor, dict(out=gsb[:], in0=MBk[:, 0:256], in1=gm[:].rearrange("p i n -> p (i n)"),
                                                     op=ALU.add), reads=[BMBk, bsetup], writes=[bg])
                        for i in range(16):
                            dve.op(V.max, dict(out=top8[:, i * 8:(i + 1) * 8], in_=gsb[:, i * 16:(i + 1) * 16]),
                                   reads=[bg], writes=[bg])
                            dve.op(V.tensor_scalar, dict(out=mb[:, i * 16:(i + 1) * 16], in0=gsb[:, i * 16:(i + 1) * 16],
                                                         scalar1=top8[:, i * 8 + 2:i * 8 + 3], scalar2=NEG,
                                                         op0=ALU.is_lt, op1=ALU.mult), reads=[bg], writes=[bg])
                        dve.op(V.tensor_scalar, dict(out=mb2[:], in0=gsb[:], scalar1=-20000.0, scalar2=NEG,
                                                     op0=ALU.is_lt, op1=ALU.mult), reads=[bg], writes=[bg])
                        dve.op(V.tensor_tensor, dict(out=mb[:], in0=mb[:], in1=mb2[:], op=ALU.add), reads=[bg], writes=[bg])
                        for i in range(16):
                            nb = i * 16 + 8 + i // 2
                            dve.op(V.memset, dict(ap=mb[:, nb:nb + 1], constant=0.0), reads=[bg], writes=[bg])
                    else:
                        for j in range(32):
                            if j < 16:
                                dve.op(V.tensor_scalar, dict(out=fb[p][:, j, :], in0=crefb[:, hh, :], scalar1=ccolT[:, j, hh:hh + 1],
                                                              scalar2=ctxm[:, 0:1], op0=ALU.subtract, op1=ALU.add),
                                       reads=[bsetup, bsmall], writes=[bfb[p]])
                            else:
                                dve.op(V.tensor_scalar, dict(out=fb[p][:, j, :], in0=crefb[:, hh, :], scalar1=ccolT[:, j, hh:hh + 1],
                                                              scalar2=None, op0=ALU.subtract),
                                       reads=[bsetup], writes=[bfb[p]])

                def prepB(hi):
                    hd = heads[hi]
                    if hd >= 16:
                        return
                    p = hi % 2
                    MBk, BMBk = MBs[p]
                    for ib in range(4):
                        for ii in range(4):
                            i = ib * 4 + ii
                            pe.op(T.matmul, dict(out=MBk[0:16, ii * 128:(ii + 1) * 128], lhsT=mb[:, i * 16:(i + 1) * 16],
                                                 rhs=identf[:], start=True, stop=True),
                                  reads=[bg, bconst], writes=[BMBk], track=(ii == 3))
                        dve.op(V.tensor_copy, dict(out=MT[p][0:16, ib * 512:(ib + 1) * 512], in_=MBk[0:16, :]),
                               reads=[BMBk], writes=[bMT[p]])

                steps = []
                for c in range(4):
                    last = 16 + 4 * c + 3
                    for j in range(last + 1):
                        steps.append((c, j, last))
                NSTEP = len(steps)
                HOOK_A = 2
                HOOK_B = 20
                LA = 3
                gsi = [0]

                def sweep(hi):
                    hd = heads[hi]
                    moba = hd < 16
                    hh = hd % 16
                    kt, vt, qt, zt, bkt, bvt, bqt, bzt = slot_of(hi)
                    p = hi % 2
                    ys = hi % 2
                    base = gsi[0]

                    def qk(si):
                        c, j, last = steps[si]
                        r0 = max(0, j - (16 + 4 * c))
                        lo = r0 * 128
                        ring = (base + si) % 4
                        sbank, bsb = SB[ring], BSB[ring]
                        q0 = c * 512 + lo
                        extra = []
                        if moba:
                            extra.append((selb[:, j // 2, :], MT[p][:, q0:(c + 1) * 512], lo, 512, [bsetup, bMT[p]]))
                            if j >= 16 + 4 * c:
                                w = 256 if r0 + 1 < 4 else 128
                                for tbt in (tbxh, tbxl):
                                    extra.append((identb[:], tbt[:, 2 * hh:2 * hh + w // 128, :].rearrange("p r t -> p (r t)"),
                                                  lo, lo + w, [bconst, btbx]))
                            elif j == 16 + 4 * c - 1:
                                for tbt in (tbxh, tbxl):
                                    extra.append((identb[:], tbt[:, 2 * hh + 1, :], 0, 128, [bconst, btbx]))
                            bias_ap = b31b[:, hh:hh + 1]
                            rb = [bsmall]
                        else:
                            if j >= 16 + 4 * c:
                                extra.append((identb[:], cmb[:], lo, lo + 128, [bconst, bsetup]))
                            bias_ap = fb[p][:, j, c:c + 1]
                            rb = [bfb[p]]
                        pe.op(T.matmul, dict(out=sbank[:, lo:512], lhsT=kt[:, j * 128:(j + 1) * 128],
                                             rhs=qt[:, q0:(c + 1) * 512], start=True, stop=(not extra)),
                              reads=[bkt, bqt], writes=[bsb], track=(not extra))
                        for ei, (l_, r_, a_, b_, rd_) in enumerate(extra):
                            lastx = ei == len(extra) - 1
                            pe.op(T.matmul, dict(out=sbank[:, a_:b_], lhsT=l_, rhs=r_, start=False, stop=lastx),
                                  reads=rd_, writes=[bsb], track=lastx)
                        pring = (base + si) % NPT
                        act.op(A.activation, dict(out=PT[pring][:, lo:512], in_=sbank[:, lo:512], func=AF.Exp,
                                                  bias=bias_ap, scale=1.0),
                               reads=[bsb] + rb, writes=[bPT[pring]])

                    def pv(si):
                        c, j, last = steps[si]
                        r0 = max(0, j - (16 + 4 * c))
                        lo = r0 * 128
                        pring = (base + si) % NPT
                        py, bpy = PY[(hi * 4 + c) % 2]
                        psm, bpsm = PSM
                        pe.op(T.matmul, dict(out=py[:, lo:512], lhsT=Vtok[p][:, j, :], rhs=PT[pring][:, lo:512],
                                             start=(j == 0), stop=(j == last)),
                              reads=[bVtok[p][j // 8], bPT[pring]], writes=[bpy], track=True)
                        DEFER = 3
                        if j >= DEFER:
                            todo = list(range(0, DEFER + 1)) if j == DEFER else [j]
                            for jj in todo:
                                prj = (base + si - (j - jj)) % NPT
                                pe.op(T.matmul, dict(out=psm[:, 0:512] if jj < DEFER + 1 and j == DEFER else psm[:, lo:512],
                                                     lhsT=onesb[:],
                                                     rhs=PT[prj][:, 0:512] if jj < DEFER + 1 and j == DEFER else PT[prj][:, lo:512],
                                                     start=(jj == 0), stop=(jj == last)),
                                      reads=[bsetup, bPT[prj]], writes=[bpsm], track=True)
                        if j == last:
                            dve.op(V.reciprocal, dict(out=rc[:], in_=psm[:]), reads=[bpsm], writes=[brc])
                            dve.op(V.tensor_tensor, dict(out=yt[:], in0=py[:], in1=rc[:], op=ALU.mult),
                                   reads=[bpy, brc], writes=[byt])
                            dve.op(V.tensor_tensor, dict(out=ygst[ys][:, c * 512:(c + 1) * 512], in0=yt[:],
                                                         in1=zt[:, c * 512:(c + 1) * 512], op=ALU.mult),
                                   reads=[byt, bzt], writes=[bygst[ys]])

                    for si in range(NSTEP + LA):
                        if si == HOOK_A and hi + 1 < NH:
                            prepA(hi + 1)
                        if si == HOOK_B and hi + 1 < NH:
                            prepB(hi + 1)
                        if si < NSTEP:
                            qk(si)
                        if si - LA >= 0:
                            pv(si - LA)
                    gsi[0] += NSTEP
                    yrow = hd * 128
                    sp.dma(YG[yrow:yrow + 128, :], ygst[ys][:], reads=[bygst[ys]])

                if NH:
                    prepA(0)
                    prepB(0)
                setup_tbx()
                nconv = 0
                for hi in range(NH):
                    if hi + 2 < NH:
                        load_head(heads[hi + 2], (hi + 2) % 3)
                    if hi >= 2 or NH < 8:
                        pool._wait(pe.sem, pe.n)
                        for _ in range(3 if NH >= 8 else 16):
                            convert_weights(nconv)
                            nconv += 1
                    sweep(hi)
                while nconv < 64:
                    convert_weights(nconv)
                    nconv += 1
                fw.barrier()
                fw.stack = st

        if 3 in phases:
            fw.barrier()
            with ExitStack() as p3:
                fw.stack = p3
                T3 = 256
                ygT = fw.sb("ygT", [128, 32, T3], BF16)
                bygT = fw.buf("ygT", dma=True)
                mT = fw.sb("mT", [128, 32, T3], BF16)
                bmT = [fw.buf(f"mT{i}") for i in range(8)]
                rr = [fw.sb(f"rr{i}", [128, D], F32) for i in range(2)]
                brr = [fw.buf(f"rr{i}", dma=True) for i in range(2)]
                gain = fw.sb("gain", [128, D], F32)
                bias = fw.sb("bias", [128, D], F32)
                bgb = fw.buf("gb", dma=True)
                wslot = [fw.sb(f"ws{i}", [128, 32, 512], BF16) for i in range(2)]
                bws = [fw.buf(f"ws{i}", dma=True) for i in range(2)]
                gt = [fw.sb(f"gt{i}", [128, 8, T3], F32) for i in range(2)]
                bgt = [fw.buf(f"gt{i}", dma=True) for i in range(2)]
                t1 = fw.sb("t1", [128, T3], F32)
                t2 = fw.sb("t2", [128, T3], F32)
                bt1 = fw.buf("t1")
                bt2 = fw.buf("t2")
                stt = fw.sb("stt", [128, 8, 6], F32)
                mv = fw.sb("mv", [128, 4], F32)
                bst = fw.buf("st")
                epsb = fw.sb("epsb", [128, 1], F32)
                dve.op(V.memset, dict(ap=epsb[:], constant=EPS), writes=[bgb])
                sp.dma(gain[:], lng, writes=[bgb])
                sp.dma(bias[:], lnb, writes=[bgb])
                NT3 = NOWN // T3
                jobs = []
                for tt in range(NT3):
                    for ct in range(8):
                        jobs.append(("br", tt, ct))
                    for ct in range(8):
                        jobs.append(("out", tt, ct))

                def load_w(n):
                    kind, tt, ct = jobs[n]
                    slot = n % 2
                    srcw = WBR16 if kind == "br" else WO16
                    sp.dma(wslot[slot][:], srcw[:, ct * 512:(ct + 1) * 512].rearrange("(w p) c -> p w c", p=128),
                           writes=[bws[slot]])

                def load_gates(n):
                    kind, tt, ct = jobs[n]
                    if kind != "br":
                        return
                    gs = (tt * 8 + ct) % 2
                    tok0 = tt * T3
                    for br in range(2):
                        r0 = br * 4096 + ct * 512
                        kw = dict(writes=[bgt[gs]]) if br == 0 else dict(free_owner=bgt[gs])
                        sp.dma(gt[gs][:, br * 4:(br + 1) * 4, :],
                               SG[r0:r0 + 512, tok0:tok0 + T3].rearrange("(g p) t -> p g t", p=128), **kw)

                def load_yg(tt):
                    tok0 = tt * T3
                    act.dma(ygT[:], YG[:, tok0:tok0 + T3].rearrange("(w p) t -> p w t", p=128), writes=[bygT])

                load_yg(0)
                load_w(0)
                load_gates(0)
                for n, (kind, tt, ct) in enumerate(jobs):
                    tok0 = tt * T3
                    slot = n % 2
                    if n + 1 < len(jobs):
                        load_w(n + 1)
                        load_gates(n + 1)
                    if kind == "br":
                        gs = (tt * 8 + ct) % 2
                        for g in range(4):
                            dc = ct * 4 + g
                            ba, bba = nextbank()
                            for wc in range(16):
                                pe.op(T.matmul, dict(out=ba[:, 0:T3], lhsT=wslot[slot][:, wc, g * 128:(g + 1) * 128],
                                                     rhs=ygT[:, wc, :], start=(wc == 0), stop=(wc == 15)),
                                      reads=[bws[slot], bygT], writes=[bba], track=(wc == 15))
                            bf_, bbf = nextbank()
                            for wc in range(16):
                                pe.op(T.matmul, dict(out=bf_[:, 0:T3], lhsT=wslot[slot][:, 16 + wc, g * 128:(g + 1) * 128],
                                                     rhs=ygT[:, 16 + wc, :], start=(wc == 0), stop=(wc == 15)),
                                      reads=[bws[slot], bygT], writes=[bbf], track=(wc == 15))
                            dve.op(V.tensor_tensor, dict(out=t1[:], in0=ba[:, 0:T3], in1=gt[gs][:, g, :], op=ALU.mult),
                                   reads=[bba, bgt[gs]], writes=[bt1])
                            dve.op(V.tensor_tensor, dict(out=t2[:], in0=bf_[:, 0:T3], in1=gt[gs][:, 4 + g, :], op=ALU.mult),
                                   reads=[bbf, bgt[gs]], writes=[bt2])
                            dve.op(V.tensor_tensor, dict(out=mT[:, dc, :], in0=t1[:], in1=t2[:], op=ALU.add),
                                   reads=[bt1, bt2], writes=[bmT[ct]])
                        continue
                    if ct == 0:
                        if tt + 1 < NT3:
                            load_yg(tt + 1)
                        for ts in range(2):
                            sp.dma(rr[ts][:], xo[tok0 + ts * 128: tok0 + (ts + 1) * 128, :], writes=[brr[ts]])
                    for ts in range(2):
                        bk, bbk = nextbank()
                        for kc in range(32):
                            pe.op(T.matmul, dict(out=bk[:], lhsT=mT[:, kc, ts * 128:(ts + 1) * 128],
                                                 rhs=wslot[slot][:, kc, :], start=(kc == 0), stop=(kc == 31)),
                                  reads=[bws[slot], bmT[kc // 4]], writes=[bbk], track=(kc == 31))
                        dve.op(V.scalar_tensor_tensor, dict(out=rr[ts][:, ct * 512:(ct + 1) * 512],
                                                            in0=rr[ts][:, ct * 512:(ct + 1) * 512], scalar=ALPHA,
                                                            in1=bk[:], op0=ALU.mult, op1=ALU.add),
                               reads=[bbk, brr[ts]], writes=[brr[ts]])
                    if ct != 7:
                        continue
                    for ts in range(2):
                        for q in range(8):
                            dve.op(V.bn_stats, dict(out=stt[:, q, :], in_=rr[ts][:, q * 512:(q + 1) * 512]),
                                   reads=[brr[ts]], writes=[bst])
                        dve.op(V.bn_aggr, dict(out=mv[:, 0:2], in_=stt[:].rearrange("p q s -> p (q s)")),
                               reads=[bst], writes=[bst])
                        act.op(A.activation, dict(out=mv[:, 2:3], in_=mv[:, 1:2], func=AF.Sqrt, bias=epsb[:, 0:1], scale=1.0),
                               reads=[bst, bgb], writes=[bst])
                        dve.op(V.reciprocal, dict(out=mv[:, 2:3], in_=mv[:, 2:3]), reads=[bst], writes=[bst])
                        dve.op(V.tensor_scalar, dict(out=rr[ts][:], in0=rr[ts][:], scalar1=mv[:, 0:1], scalar2=mv[:, 2:3],
                                                     op0=ALU.subtract, op1=ALU.mult), reads=[bst, brr[ts]], writes=[brr[ts]])
                        dve.op(V.tensor_tensor, dict(out=rr[ts][:], in0=rr[ts][:], in1=gain[:], op=ALU.mult),
                               reads=[brr[ts], bgb], writes=[brr[ts]])
                        dve.op(V.tensor_tensor, dict(out=rr[ts][:], in0=rr[ts][:], in1=bias[:], op=ALU.add),
                               reads=[brr[ts], bgb], writes=[brr[ts]])
                        pool.dma(out[tok0 + ts * 128: tok0 + (ts + 1) * 128, :], rr[ts][:], reads=[brr[ts]])
                fw.barrier()
                fw.stack = st
        else:
            pass

        fw.barrier()
        fw.replay()
    return nc


def _t5_bucket(d):
    d = np.asarray(d, np.int64)
    df = np.maximum(d, 1).astype(np.float32)
    large = 16 + (np.log(df / np.float32(16)) / np.float32(np.log(128 / 16)) * np.float32(16)).astype(np.int32)
    large = np.minimum(large, 31)
    return np.where(d < 16, d, large)


def make_in_maps(x, w_in, b_forget, b_gate, rel_bias_table, w_branch, w_out, ln_gain, ln_bias, cores=range(8)):
    f32 = np.float32
    x = np.asarray(x, f32)
    w_in2 = np.ascontiguousarray(np.asarray(w_in, f32)[0])
    w_br = np.ascontiguousarray(np.asarray(w_branch, f32)[0])
    w_o = np.ascontiguousarray(np.asarray(w_out, f32)[0])
    table = np.asarray(rel_bias_table, f32)
    s = np.arange(128)[:, None]
    t = np.arange(128)[None, :]
    bd = _t5_bucket(np.maximum(t - s, 0))
    bo = _t5_bucket(128 + t - s)
    tb = np.stack([np.stack([table[bd, h], table[bo, h]], 0) for h in range(16)], 0).astype(f32)
    b31 = np.ascontiguousarray(np.broadcast_to(table[31][None, :], (128, 16))).astype(f32)
    bfT = np.ascontiguousarray(np.asarray(b_forget, f32)[0].reshape(16, 1))
    bgT = np.ascontiguousarray(np.asarray(b_gate, f32)[0].reshape(64, 128).T)
    lng = np.ascontiguousarray(np.broadcast_to(np.asarray(ln_gain, f32)[0][None, :], (128, D)))
    lnb = np.ascontiguousarray(np.broadcast_to(np.asarray(ln_bias, f32)[0][None, :], (128, D)))
    maps = []
    for c in cores:
        b, h = c // 2, c % 2
        xo = np.ascontiguousarray(x[b, h * NOWN:(h + 1) * NOWN])
        xc = np.ascontiguousarray(x[b, 0:NCTX] if h == 1 else x[b, NOWN:NOWN + NCTX])
        ctxm = np.full((128, 1), 0.0 if h == 1 else NEG, f32)
        maps.append({"xo": xo, "xc": xc, "w_in": w_in2, "bfT": bfT, "bgT": bgT, "tb": tb, "b31": b31,
                     "w_br": w_br, "w_out": w_o, "lng": lng, "lnb": lnb, "ctxm": ctxm})
    return maps


def kernel(x, w_in, b_forget, b_gate, rel_bias_table, w_branch, w_out, ln_gain, ln_bias):
    maps = make_in_maps(x, w_in, b_forget, b_gate, rel_bias_table, w_branch, w_out, ln_gain, ln_bias)
    nc = build_program()
    res = run_bass_kernel_spmd(nc, maps, core_ids=list(range(8)))
    outp = np.empty((4, 4096, D), np.float32)
    for c in range(8):
        b, h = c // 2, c % 2
        outp[b, h * NOWN:(h + 1) * NOWN] = res.results[c]["out"]
    return outp
```
